# Optimizing a Trainium2 kernel written in Bass

```python
import jax, jax.numpy as jnp
from jax import lax
import numpy as np

D_MODEL = 1024
BATCH = 2
SEQ = 8192
DEPTH = 4
DEC_BATCH = 128
DEC_SEQ = 1
PAST_LEN = 8192
PAGE_SIZE = 128

N_EVEN = (DEPTH + 1) // 2
N_ODD = DEPTH // 2
CHUNK = 128
A_WIDTH = D_MODEL
A_GROUPS = 4
B_WIDTH = D_MODEL
CONV_WIDTH = 31
MLA_HEADS = 8
Q_LORA = D_MODEL // 2
KV_LORA = D_MODEL // 4
QK_NOPE = 128
QK_ROPE = 64
V_HEAD = 128
ROPE_THETA = 10000.0
ATTN_SCALE = (QK_NOPE + QK_ROPE) ** -0.5
FFN_HIDDEN = ((8 * D_MODEL // 3 + 255) // 256) * 256
EPS = 1e-6

kernel_name = "hybrid_gmlp_conformer_mla_decode_step"


def rmsnorm(x, g):
    xf = x.astype(jnp.float32)
    y = xf * lax.rsqrt(jnp.mean(xf * xf, axis=-1, keepdims=True) + EPS)
    return (y * g.astype(jnp.float32)).astype(x.dtype)


def layernorm(x, g, b):
    xf = x.astype(jnp.float32)
    mu = jnp.mean(xf, axis=-1, keepdims=True)
    var = jnp.mean(jnp.square(xf - mu), axis=-1, keepdims=True)
    y = (xf - mu) * lax.rsqrt(var + EPS)
    return (y * g.astype(jnp.float32) + b.astype(jnp.float32)).astype(x.dtype)


def rope(x, pos):
    half = QK_ROPE // 2
    inv = ROPE_THETA ** (-jnp.arange(half, dtype=jnp.float32) / half)
    ang = pos.astype(jnp.float32)[:, None] * inv[None, :]
    shape = (1, pos.shape[0]) + (1,) * (x.ndim - 3) + (half,)
    c = jnp.cos(ang).reshape(shape)
    s = jnp.sin(ang).reshape(shape)
    xf = x.astype(jnp.float32)
    x1, x2 = xf[..., :half], xf[..., half:]
    return jnp.concatenate([x1 * c - x2 * s, x1 * s + x2 * c], axis=-1).astype(x.dtype)


def chunk_spatial_gate(v, w_s, b_s):
    bsz, t, _ = v.shape
    length = min(t, CHUNK)
    n = -(-t // length)
    pad = n * length - t
    vp = jnp.pad(v, ((0, 0), (0, pad), (0, 0))) if pad else v
    vp = vp.reshape(bsz, n, length, A_GROUPS, A_WIDTH // A_GROUPS)
    mask = jnp.tril(jnp.ones((length, length), dtype=bool))
    w = jnp.where(mask[None], w_s[:, :length, :length], 0.0).astype(v.dtype)
    s = jnp.einsum('gij,bnjgc->bnigc', w, vp) + b_s[:, :length].T[None, None, :, :, None]
    return s.reshape(bsz, n * length, A_WIDTH)[:, :t]


def causal_depthwise_conv(x, prefix, w, b):
    xe = jnp.concatenate([prefix, x], axis=1)
    y = lax.conv_general_dilated(xe, w[:, None, :].astype(xe.dtype), window_strides=(1,), padding='VALID',
                                 dimension_numbers=('NWC', 'WIO', 'NWC'),
                                 feature_group_count=x.shape[-1])
    return y + b, xe[:, -(CONV_WIDTH - 1):]


def even_mixer(h, conv_prefix, w_in, ln_v_g, ln_v_b, w_s, b_s, conv_w, conv_b, ln_c_g, ln_c_b, w_out):
    z = h @ w_in
    zu, zv, zb, zg = jnp.split(z, [A_WIDTH, 2 * A_WIDTH, 2 * A_WIDTH + B_WIDTH], axis=-1)
    u = jax.nn.gelu(zu)
    v = layernorm(jax.nn.gelu(zv), ln_v_g, ln_v_b)
    a_out = u * chunk_spatial_gate(v, w_s, b_s)
    glu = zb * jax.nn.sigmoid(zg)
    c, new_buf = causal_depthwise_conv(glu, conv_prefix, conv_w, conv_b)
    b_out = jax.nn.silu(layernorm(c, ln_c_g, ln_c_b))
    y = jnp.concatenate([a_out, b_out], axis=-1) @ w_out
    return y, new_buf, v


def mla_project(h, pos, w_in_c, q_norm, kv_norm, w_uq):
    bsz, t, _ = h.shape
    z = h @ w_in_c
    cq, ckv, kpe = jnp.split(z, [Q_LORA, Q_LORA + KV_LORA], axis=-1)
    cq = rmsnorm(cq, q_norm)
    ckv = rmsnorm(ckv, kv_norm)
    kpe = rope(kpe, pos)
    q = (cq @ w_uq).reshape(bsz, t, MLA_HEADS, QK_NOPE + QK_ROPE)
    q_nope = q[..., :QK_NOPE]
    q_pe = rope(q[..., QK_NOPE:], pos)
    return q_nope, q_pe, ckv, kpe


def mla_prompt(h, w_in_c, q_norm, kv_norm, w_uq, w_uk, w_uv, w_out_c):
    bsz, t, _ = h.shape
    pos = jnp.arange(t)
    q_nope, q_pe, ckv, kpe = mla_project(h, pos, w_in_c, q_norm, kv_norm, w_uq)
    k_nope = jnp.einsum('btr,rhn->bthn', ckv, w_uk)
    v = jnp.einsum('btr,rhv->bthv', ckv, w_uv)
    qb = min(t, CHUNK)
    nb = t // qb
    qn = q_nope.reshape(bsz, nb, qb, MLA_HEADS, QK_NOPE).transpose(1, 0, 2, 3, 4)
    qp = q_pe.reshape(bsz, nb, qb, MLA_HEADS, QK_ROPE).transpose(1, 0, 2, 3, 4)
    kpos = jnp.arange(t)

    def block(args):
        qn_b, qp_b, start = args
        s = (jnp.einsum('bqhn,bkhn->bhqk', qn_b, k_nope)
             + jnp.einsum('bqhp,bkp->bhqk', qp_b, kpe)).astype(jnp.float32) * ATTN_SCALE
        qpos = start + jnp.arange(qb)
        s = jnp.where(kpos[None, :] <= qpos[:, None], s, -jnp.inf)
        p = jax.nn.softmax(s, axis=-1).astype(v.dtype)
        return jnp.einsum('bhqk,bkhv->bqhv', p, v)

    o = lax.map(block, (qn, qp, jnp.arange(nb) * qb))
    o = o.transpose(1, 0, 2, 3, 4).reshape(bsz, t, MLA_HEADS * V_HEAD)
    return o @ w_out_c, ckv, kpe


def mla_sample(h, ckv_pool, kpe_pool, layer, page_table, w_in_c, q_norm, kv_norm, w_uq, w_uk, w_uv, w_out_c):
    bsz, s_len, _ = h.shape
    past = page_table.shape[1] * ckv_pool.shape[2]
    pos = past + jnp.arange(s_len)
    q_nope, q_pe, ckv, kpe = mla_project(h, pos, w_in_c, q_norm, kv_norm, w_uq)
    q_lat = jnp.einsum('bshn,rhn->bshr', q_nope, w_uk)
    ckv_past = ckv_pool[layer, page_table].reshape(bsz, past, KV_LORA)
    kpe_past = kpe_pool[layer, page_table].reshape(bsz, past, QK_ROPE)
    s_past = (jnp.einsum('bshr,bkr->bhsk', q_lat, ckv_past)
              + jnp.einsum('bshp,bkp->bhsk', q_pe, kpe_past)).astype(jnp.float32) * ATTN_SCALE
    s_new = (jnp.einsum('bshr,bkr->bhsk', q_lat, ckv)
             + jnp.einsum('bshp,bkp->bhsk', q_pe, kpe)).astype(jnp.float32) * ATTN_SCALE
    causal = jnp.arange(s_len)[None, :] <= jnp.arange(s_len)[:, None]
    s = jnp.concatenate([s_past, jnp.where(causal, s_new, -jnp.inf)], axis=-1)
    p = jax.nn.softmax(s, axis=-1).astype(ckv.dtype)
    o_lat = (jnp.einsum('bhsk,bkr->bshr', p[..., :past], ckv_past)
             + jnp.einsum('bhsk,bkr->bshr', p[..., past:], ckv))
    o = jnp.einsum('bshr,rhv->bshv', o_lat, w_uv).reshape(bsz, s_len, MLA_HEADS * V_HEAD)
    return o @ w_out_c, ckv, kpe


def swiglu(h, wg, wu, wd):
    return (jax.nn.silu(h @ wg) * (h @ wu)) @ wd


def setup_inputs(seed: int = 0) -> dict:
    key = jax.random.key(seed)
    ks = jax.random.split(key, 32)

    def nrm(k, shape, scale):
        return jax.random.normal(k, shape, jnp.float32) * scale

    n_pages = PAST_LEN // PAGE_SIZE
    n_used = DEC_BATCH * n_pages
    n_pool = n_used + n_used // 4
    page_table = jax.random.permutation(ks[5], n_pool)[:n_used].reshape(DEC_BATCH, n_pages).astype(jnp.int32)
    ab_in = 2 * A_WIDTH + 2 * B_WIDTH
    c_in = Q_LORA + KV_LORA + QK_ROPE
    return {
        "x_prompt": nrm(ks[0], (BATCH, SEQ, D_MODEL), 1.0),
        "x_sample": nrm(ks[1], (DEC_BATCH, DEC_SEQ, D_MODEL), 1.0),
        "cache_ckv": nrm(ks[2], (N_ODD, n_pool, PAGE_SIZE, KV_LORA), 1.0),
        "cache_kpe": nrm(ks[3], (N_ODD, n_pool, PAGE_SIZE, QK_ROPE), 1.0),
        "page_table": page_table,
        "state_conv": nrm(ks[4], (N_EVEN, DEC_BATCH, CONV_WIDTH - 1, B_WIDTH), 0.5),
        "norm_mix": 1.0 + nrm(ks[6], (DEPTH, D_MODEL), 0.02),
        "norm_ffn": 1.0 + nrm(ks[7], (DEPTH, D_MODEL), 0.02),
        "norm_final": 1.0 + nrm(ks[8], (D_MODEL,), 0.02),
        "w_in_ab": nrm(ks[9], (N_EVEN, D_MODEL, ab_in), D_MODEL ** -0.5),
        "gmlp_ln_g": 1.0 + nrm(ks[10], (N_EVEN, A_WIDTH), 0.02),
        "gmlp_ln_b": nrm(ks[11], (N_EVEN, A_WIDTH), 0.02),
        "w_spatial": nrm(ks[12], (N_EVEN, A_GROUPS, CHUNK, CHUNK), CHUNK ** -0.5),
        "b_spatial": 1.0 + nrm(ks[13], (N_EVEN, A_GROUPS, CHUNK), 0.02),
        "conv_w": nrm(ks[14], (N_EVEN, CONV_WIDTH, B_WIDTH), CONV_WIDTH ** -0.5),
        "conv_b": nrm(ks[15], (N_EVEN, B_WIDTH), 0.02),
        "conv_ln_g": 1.0 + nrm(ks[16], (N_EVEN, B_WIDTH), 0.02),
        "conv_ln_b": nrm(ks[17], (N_EVEN, B_WIDTH), 0.02),
        "w_out_ab": nrm(ks[18], (N_EVEN, A_WIDTH + B_WIDTH, D_MODEL), (A_WIDTH + B_WIDTH) ** -0.5),
        "w_in_c": nrm(ks[19], (N_ODD, D_MODEL, c_in), D_MODEL ** -0.5),
        "q_norm": 1.0 + nrm(ks[20], (N_ODD, Q_LORA), 0.02),
        "kv_norm": 1.0 + nrm(ks[21], (N_ODD, KV_LORA), 0.02),
        "w_uq": nrm(ks[22], (N_ODD, Q_LORA, MLA_HEADS * (QK_NOPE + QK_ROPE)), Q_LORA ** -0.5),
        "w_uk": nrm(ks[23], (N_ODD, KV_LORA, MLA_HEADS, QK_NOPE), KV_LORA ** -0.5),
        "w_uv": nrm(ks[24], (N_ODD, KV_LORA, MLA_HEADS, V_HEAD), KV_LORA ** -0.5),
        "w_out_c": nrm(ks[25], (N_ODD, MLA_HEADS * V_HEAD, D_MODEL), (MLA_HEADS * V_HEAD) ** -0.5),
        "w_gate": nrm(ks[26], (DEPTH, D_MODEL, FFN_HIDDEN), D_MODEL ** -0.5),
        "w_up": nrm(ks[27], (DEPTH, D_MODEL, FFN_HIDDEN), D_MODEL ** -0.5),
        "w_down": nrm(ks[28], (DEPTH, FFN_HIDDEN, D_MODEL), FFN_HIDDEN ** -0.5),
    }


def reference(x_prompt, x_sample, cache_ckv, cache_kpe, page_table, state_conv,
              norm_mix, norm_ffn, norm_final,
              w_in_ab, gmlp_ln_g, gmlp_ln_b, w_spatial, b_spatial, conv_w, conv_b, conv_ln_g, conv_ln_b, w_out_ab,
              w_in_c, q_norm, kv_norm, w_uq, w_uk, w_uv, w_out_c,
              w_gate, w_up, w_down):
    xp, xs = x_prompt, x_sample
    ckv_p, kpe_p, ckv_s, kpe_s = [], [], [], []
    conv_p, conv_s, v_s = [], [], []
    for layer in range(DEPTH):
        hp = rmsnorm(xp, norm_mix[layer])
        hs = rmsnorm(xs, norm_mix[layer])
        if layer % 2 == 0:
            e = layer // 2
            ew = (w_in_ab[e], gmlp_ln_g[e], gmlp_ln_b[e], w_spatial[e], b_spatial[e],
                  conv_w[e], conv_b[e], conv_ln_g[e], conv_ln_b[e], w_out_ab[e])
            zero_prefix = jnp.zeros((xp.shape[0], CONV_WIDTH - 1, B_WIDTH), xp.dtype)
            yp, buf_p, _ = even_mixer(hp, zero_prefix, *ew)
            ys, buf_s, vrows = even_mixer(hs, state_conv[e], *ew)
            conv_p.append(buf_p)
            conv_s.append(buf_s)
            v_s.append(vrows)
        else:
            o = layer // 2
            cw = (w_in_c[o], q_norm[o], kv_norm[o], w_uq[o], w_uk[o], w_uv[o], w_out_c[o])
            yp, ckv_new_p, kpe_new_p = mla_prompt(hp, *cw)
            ys, ckv_new_s, kpe_new_s = mla_sample(hs, cache_ckv, cache_kpe, o, page_table, *cw)
            ckv_p.append(ckv_new_p)
            kpe_p.append(kpe_new_p)
            ckv_s.append(ckv_new_s)
            kpe_s.append(kpe_new_s)
        xp = xp + yp
        xs = xs + ys
        xp = xp + swiglu(rmsnorm(xp, norm_ffn[layer]), w_gate[layer], w_up[layer], w_down[layer])
        xs = xs + swiglu(rmsnorm(xs, norm_ffn[layer]), w_gate[layer], w_up[layer], w_down[layer])
    y_prompt = rmsnorm(xp, norm_final)
    y_sample = rmsnorm(xs, norm_final)
    new_ckv_prompt = jnp.stack(ckv_p)
    new_kpe_prompt = jnp.stack(kpe_p)
    new_ckv_sample = jnp.stack(ckv_s)
    new_kpe_sample = jnp.stack(kpe_s)
    new_conv_prompt = jnp.stack(conv_p)
    new_conv_sample = jnp.stack(conv_s)
    new_gmlp_v_sample = jnp.stack(v_s)
    return (y_prompt, y_sample, new_ckv_prompt, new_kpe_prompt, new_ckv_sample, new_kpe_sample,
            new_conv_prompt, new_conv_sample, new_gmlp_v_sample)
```

```python
import math
from contextlib import ExitStack
import numpy as np
import concourse.bass as bass
import concourse.mybir as mybir
from concourse.bass_utils import run_bass_kernel_spmd

F32 = mybir.dt.float32
BF16 = mybir.dt.bfloat16
I32 = mybir.dt.int32
AF = mybir.ActivationFunctionType
ALU = mybir.AluOpType

D = 1024
EPS = 1e-6
SCALE = 192.0 ** -0.5
FH = 2816
NJ = 22
USE_ACT_GELU = True


class Buf:
    def __init__(self, name):
        self.name = name
        self.w = {}
        self.r = {}
        self.dsem = None
        self.dcnt = 0


class Eng:
    def __init__(self, name):
        self.name = name
        self.sem = None
        self.cnt = 0
        self.seen = {}
        self.ops = []


class T:
    def __init__(self, t, units):
        self.t = t
        self.u = units

    def __getitem__(self, k):
        return self.t[k]


class Prog:
    def __init__(self, nc, stack):
        self.nc = nc
        self.stack = stack
        self.eng = {n: Eng(n) for n in ("pe", "act", "dve", "pool", "sp")}
        self.sems = {}
        for n, e in self.eng.items():
            e.sem = self.newsem("e_" + n)
        self.dry = False
        self.out_ev = {}
        self.nsem = 0

    def newsem(self, name):
        s = self.stack.enter_context(self.nc.semaphore(name))
        self.sems[id(s)] = s
        return s

    def sb(self, name, shape, dt, units=True):
        t = self.stack.enter_context(self.nc.sbuf_tensor(name, list(shape), dt))
        if units and len(shape) >= 3:
            u = [Buf("%s[%d]" % (name, i)) for i in range(shape[1])]
        else:
            u = [Buf(name)]
        return T(t, u)

    def _deps(self, e, reads, writes):
        waits = {}

        def need(d):
            for k, (sm, v) in d.items():
                if e.name == "pe" and sm is e.sem:
                    continue
                if e.seen.get(k, 0) < v:
                    if k not in waits or waits[k][1] < v:
                        waits[k] = (sm, v)
        for b in reads:
            need(b.w)
        for b in writes:
            need(b.w)
            need(b.r)
        for k, (sm, v) in waits.items():
            e.seen[k] = v
        return list(waits.values())

    def op(self, en, fn, reads=(), writes=()):
        if self.dry:
            return
        e = self.eng[en]
        waits = self._deps(e, reads, writes)
        e.cnt += 1
        e.ops.append((waits, fn, e.sem, 1))
        k = id(e.sem)
        for b in reads:
            b.r[k] = (e.sem, e.cnt)
        for b in writes:
            b.w = {k: (e.sem, e.cnt)}
            b.r = {}

    def dma(self, qn, fn, own, reads=(), writes=(), dreads=(), dappend=(), is_out=False):
        if self.dry:
            return
        e = self.eng[qn]
        wr = list(writes)
        if own not in wr:
            wr.append(own)
        rd = [b for b in reads if b is not own] + list(dreads)
        waits = self._deps(e, rd, wr)
        if own.dsem is None:
            own.dsem = self.newsem("d%d" % self.nsem)
            self.nsem += 1
        own.dcnt += 1
        k = id(own.dsem)
        ev = (own.dsem, 16 * own.dcnt)
        e.ops.append((waits, fn, own.dsem, 16))
        for b in reads:
            b.r[k] = ev
        for b in writes:
            b.w = {k: ev}
            b.r = {}
        if own not in reads and own not in writes:
            own.r[k] = ev
        for d in dappend:
            d.w[k] = ev
        if is_out:
            self.out_ev[k] = ev

    def finish(self):
        e = self.eng["sp"]
        waits = [ev for k, ev in self.out_ev.items()]
        e.ops.append((waits, None, None, 0))

    def emit(self, block):
        def mk(en):
            def f(h):
                for waits, fn, sem, inc in self.eng[en].ops:
                    for sm, v in waits:
                        h.wait_ge(sm, v)
                    if fn is not None:
                        ins = fn(h)
                        ins.then_inc(sem, inc)
            return f
        block.tensor(mk("pe"))
        block.scalar(mk("act"))
        block.vector(mk("dve"))
        block.gpsimd(mk("pool"))
        block.sync(mk("sp"))


class WStream:
    def __init__(self, P, nslots, depth, elems):
        self.P = P
        self.slots = [P.sb("wslot%d" % i, [128, elems], BF16) for i in range(nslots)]
        self.depth = depth
        self.seq = []
        self.pos = 0
        self.issued = 0

    def get(self, src, shape):
        if self.P.dry:
            self.seq.append((src, shape))
            sl = self.slots[0]
            p, a, b = shape
            return sl.t[:p, 0:a * b].rearrange("p (a b) -> p a b", a=a), sl.u[0]
        i = self.pos
        self.pos += 1
        while self.issued < min(len(self.seq), i + 1 + self.depth):
            self._issue(self.issued)
            self.issued += 1
        sl = self.slots[i % len(self.slots)]
        p, a, b = shape
        return sl.t[:p, 0:a * b].rearrange("p (a b) -> p a b", a=a), sl.u[0]

    def _issue(self, i):
        src, (p, a, b) = self.seq[i]
        sl = self.slots[i % len(self.slots)]
        dst = sl.t[:p, 0:a * b].rearrange("p (a b) -> p a b", a=a)
        self.P.dma("pool", lambda h, dst=dst, src=src: h.dma_start(out=dst, in_=src), sl.u[0], writes=[sl.u[0]])


def build_nc(SEQ, NS, NPG, NPOOL):
    TT = 512
    NTILE = SEQ // TT
    nc = bass.Bass("TRN2", target_bir_lowering=False)

    def din(name, shape, dt=F32):
        return nc.dram_tensor(name, list(shape), dt, kind="ExternalInput").ap()

    def dout(name, shape, dt=F32):
        return nc.dram_tensor(name, list(shape), dt, kind="ExternalOutput").ap()

    xp = din("xp", [SEQ, D])
    xs = din("xs", [NS, D])
    cpool = din("cpool", [2 * NPOOL * 128, 320])
    ptab = din("ptab", [1, NS * NPG], I32)
    stc = din("stc", [2, NS * 30, D])
    vecs = din("vecs", [78, D])
    gmg = din("gmg", [2, D])
    gmb = din("gmb", [2, D])
    kvn = din("kvn", [2, 256])
    wsp = din("wsp", [2, 4, 128, 128])
    bsp = din("bsp", [1, 2 * 512])
    w00 = din("w00", [1, 8])
    b00 = din("b00", [1, 8])
    w_in_ab = din("w_in_ab", [2, D, 4096])
    w_out_ab = din("w_out_ab", [2, 2048, D])
    w_in_c = din("w_in_c", [2, D, 832])
    w_uq = din("w_uq", [2, 512, 1536])
    w_uk = din("w_uk", [2, 256, 1024])
    w_uv = din("w_uv", [2, 256, 1024])
    w_out_c = din("w_out_c", [2, D, D])
    w_gate = din("w_gate", [4, D, FH])
    w_up = din("w_up", [4, D, FH])
    w_down = din("w_down", [4, FH, D])
    c_ident = din("c_ident", [128, 128])
    c_tri = din("c_tri", [128, 128])
    c_cs_tm = din("c_cs_tm", [SEQ, 64])
    c_cs_fm = din("c_cs_fm", [2, 64, SEQ])
    c_cs_s = din("c_cs_s", [1, 64])
    c_cs_fs = din("c_cs_fs", [2, 64, NS])

    y_p = dout("y_p", [SEQ, D])
    y_s = dout("y_s", [NS, D])
    o_ckv_p = dout("o_ckv_p", [2, SEQ, 256])
    o_kpe_p = dout("o_kpe_p", [2, SEQ, 64])
    o_ckv_s = dout("o_ckv_s", [2, NS, 256])
    o_kpe_s = dout("o_kpe_s", [2, NS, 64])
    o_conv_p = dout("o_conv_p", [2, 30, D])
    o_conv_s = dout("o_conv_s", [2, NS * 30, D])
    o_v_s = dout("o_v_s", [2, NS, D])

    Kn_d = nc.dram_tensor("Kn_d", [2, 8, 128, SEQ], BF16).ap()
    Kp_d = nc.dram_tensor("Kp_d", [2, 64, SEQ], BF16).ap()
    V_d = nc.dram_tensor("V_d", [2, SEQ, D], BF16).ap()
    dKn = [Buf("dKn0"), Buf("dKn1")]
    dKp = [Buf("dKp0"), Buf("dKp1")]
    dV = [Buf("dV0"), Buf("dV1")]

    stack = ExitStack()
    with stack:
        P = Prog(nc, stack)
        X = P.sb("X", [128, 8, TT], F32)
        H = P.sb("H", [128, 8, TT], BF16)
        SQ = P.sb("SQ", [128, 2, TT], BF16)
        A = P.sb("A", [128, NJ, TT], BF16)
        ACC = P.sb("ACC", [128, 8, TT], F32)
        XE = P.sb("XE", [128, 2, 64], F32)
        XEB = P.sb("XEB", [128, 2, 30 + TT], BF16)
        DG = P.sb("DG", [128, 2, 16 * 128], BF16)
        CARRY = P.sb("CARRY", [128, 2, 8 * 30], F32)
        BO = P.sb("BO", [128, 8, TT], BF16)
        G = P.sb("G", [128, 2, 1024], F32)
        T1 = P.sb("T1", [128, 2, TT], F32)
        SG = P.sb("SG", [128, 2, TT], F32)
        TOK = P.sb("TOK", [128, 4, 320], F32)
        STAT = P.sb("STAT", [128, 3, TT], F32)
        SM = P.sb("SM", [128, 8, 8], F32)
        GLN = P.sb("GLN", [128, 2, 1024], F32)
        KVN = P.sb("KVN", [128, 2, 256], F32, units=False)
        PTb = P.sb("PTb", [128, 2, TT], BF16)
        KSEG = 512
        KN = P.sb("KN", [128, 2, KSEG], BF16)
        VV = P.sb("VV", [128, 2, KSEG], BF16)
        KP = P.sb("KP", [128, 2, KSEG], BF16)
        CSF = P.sb("CSF", [64, 2, TT], F32)
        CST = P.sb("CST", [128, 4, 64], F32, units=False)
        VECT = P.sb("VECT", [128, 8, 78], F32, units=False)
        WST = P.sb("WST", [128, 8, 128], BF16, units=False)
        W00I = P.sb("W00I", [16, 8, 16], BF16, units=False)
        BC8 = P.sb("BC8", [128, 2, 8], F32, units=False)
        BSP = P.sb("BSP", [1, 512], F32, units=False)
        ONE1 = P.sb("ONE1", [1, 128], F32, units=False)
        IDENT = P.sb("IDENT", [128, 128], F32, units=False)
        IDENTB = P.sb("IDENTB", [128, 128], BF16, units=False)
        TRI = P.sb("TRI", [128, 128], F32, units=False)
        TRIB = P.sb("TRIB", [128, 128], BF16, units=False)
        ONESB = P.sb("ONESB", [128, 128], BF16, units=False)
        WR = P.sb("WR", [128, 4, 256], BF16, units=False)
        STG = P.sb("STG", [128, 2, 1024], F32)
        VST = P.sb("VST", [128, 2, 1024], BF16)
        IOTAF = P.sb("IOTAF", [128, 1], F32, units=False)
        DMY = Buf("dmy")
        RSC = P.sb("RSC", [128, 64], F32, units=False)
        WUKT = T(VST.t[:, :, :].rearrange("p a (b c) -> p (a b) c", c=256), VST.u)
        GP = 2
        NCK = 5
        CK = P.sb("CK", [128, NCK, GP * 324], BF16)
        KT = P.sb("KT", [128, 3, 3 * GP * 128], BF16)
        IDX = P.sb("IDX", [128, 2, NPG], I32)
        IOTA = P.sb("IOTA", [128, 1], I32, units=False)
        PSS = P.sb("PSS", [128, 2, GP * 8], BF16)
        ONEC = P.sb("ONEC", [128, 2], BF16, units=False)
        PN = P.sb("PN", [1, 2, 8], F32)
        NEWROW = P.sb("NEWROW", [1, 2, 260], F32)
        QL = P.sb("QL", [128, 2 * NS * 8], BF16, units=False)
        QPS = P.sb("QPS", [64, NS * 8], BF16, units=False)
        OL = P.sb("OL", [8, 2, 256], F32)
        OLT = P.sb("OLT", [128, 2 * 8 * NS], BF16, units=False)
        ws = WStream(P, 4, 2, 4096)
        PS = []
        for i in range(8):
            t = stack.enter_context(nc.psum_tensor("ps%d" % i, [128, 512], F32))
            PS.append(T(t, [Buf("ps%d" % i)]))
        pst = {"i": 0}

        def psum():
            i = pst["i"]
            pst["i"] = (i + 1) % 4
            return PS[4 + i]

        def mm(ps, rows, cols, pairs, reads, start=True, stop=True):
            def fn(h, ps=ps, rows=rows, cols=cols, pairs=pairs):
                ins = None
                n = len(pairs)
                for i, (l, r) in enumerate(pairs):
                    ins = h.matmul(ps.t[rows, cols], l, r, start=(start and i == 0), stop=(stop and i == n - 1))
                return ins
            P.op("pe", fn, reads=reads, writes=ps.u)

        def act(out, in_, func, reads, writes, scale=1.0, bias=0.0, accum=None):
            def fn(h):
                kw = {}
                if accum is not None:
                    kw["accum_out"] = accum
                return h.activation(out=out, in_=in_, func=func, bias=bias, scale=scale, **kw)
            P.op("act", fn, reads=reads, writes=writes)

        def tt(out, a, b, op, reads, writes, eng="dve"):
            P.op(eng, lambda h: h.tensor_tensor(out=out, in0=a, in1=b, op=op), reads=reads, writes=writes)

        def ts(out, a, s1, s2, op0, op1, reads, writes, eng="dve"):
            if op1 is None:
                P.op(eng, lambda h: h.tensor_scalar(out=out, in0=a, scalar1=s1, scalar2=None, op0=op0), reads=reads, writes=writes)
            else:
                P.op(eng, lambda h: h.tensor_scalar(out=out, in0=a, scalar1=s1, scalar2=s2, op0=op0, op1=op1), reads=reads, writes=writes)

        def stt(out, a, s, b, op0, op1, reads, writes, eng="dve"):
            P.op(eng, lambda h: h.scalar_tensor_tensor(out=out, in0=a, scalar=s, in1=b, op0=op0, op1=op1), reads=reads, writes=writes)

        def copy(out, in_, reads, writes, eng="act"):
            if eng == "act":
                P.op("act", lambda h: h.copy(out=out, in_=in_), reads=reads, writes=writes)
            else:
                P.op(eng, lambda h: h.tensor_copy(out=out, in_=in_), reads=reads, writes=writes)

        def recip(out, in_, reads, writes):
            P.op("dve", lambda h: h.reciprocal(out=out, in_=in_), reads=reads, writes=writes)

        def dma(q, out, in_, own, **kw):
            P.dma(q, lambda h: h.dma_start(out=out, in_=in_), own, **kw)

        def gelu_to(out, ps, rows, n, wunits):
            src = ps.t[rows, 0:n]
            if USE_ACT_GELU:
                act(out, src, AF.Gelu_apprx_tanh, ps.u, wunits)
                return
            k = gelu_to.k = (getattr(gelu_to, "k", 0) + 1) % 2
            t1 = T1.t[rows, k, 0:n]
            act(t1, src, AF.Square, ps.u, [T1.u[k]])
            ts(t1, t1, 0.044715, 1.0, ALU.mult, ALU.add, [T1.u[k]], [T1.u[k]])
            tt(t1, t1, src, ALU.mult, [T1.u[k]] + ps.u, [T1.u[k]])
            act(t1, t1, AF.Sigmoid, [T1.u[k]], [T1.u[k]], scale=1.5957691216057308)
            tt(out, t1, src, ALU.mult, [T1.u[k]] + ps.u, wunits)

        def rms_fm(Xt, nch, gain, Ht, hoff, n, dim):
            ps = psum()
            for c in range(nch):
                k = c % 2
                act(SQ.t[:, k, 0:n], Xt.t[:, c, 0:n], AF.Square, [Xt.u[c]], [SQ.u[k]])
                mm(ps, slice(0, 128), slice(0, n), [(ONESB.t[:, :], SQ.t[:, k, 0:n])], [SQ.u[k]] + ONESB.u,
                   start=(c == 0), stop=(c == nch - 1))
            rs = STAT.t[:, 0, 0:n]
            act(rs, ps.t[:, 0:n], AF.Sqrt, ps.u, [STAT.u[0]], scale=1.0 / dim, bias=EPS)
            recip(rs, rs, [STAT.u[0]], [STAT.u[0]])
            for c in range(nch):
                stt(Ht.t[:, hoff + c, 0:n], Xt.t[:, c, 0:n], gain(c), rs, ALU.mult, ALU.mult,
                    [Xt.u[c], STAT.u[0]] + VECT.u, [Ht.u[hoff + c]])

        def wcols(w2d, c0, c1, kch):
            return w2d[:, c0:c1].rearrange("(k p) c -> p k c", p=128), (128, kch, c1 - c0)

        def fm_to_tm(src_fn, src_reads, nch, rows, dst, q="sp", is_out=True):
            k = fm_to_tm.k = (getattr(fm_to_tm, "k", 0) + 1) % 2
            for half in range((nch + 3) // 4):
                ps = psum()
                cs = list(range(half * 4, min(nch, half * 4 + 4)))
                for i, c in enumerate(cs):
                    mm(ps, slice(0, rows), slice(i * 128, (i + 1) * 128), [(src_fn(c), IDENT.t[:, :])],
                       src_reads(c) + IDENT.u)
                copy(STG.t[0:rows, k, half * 512:half * 512 + 128 * len(cs)], ps.t[0:rows, 0:128 * len(cs)],
                     ps.u, [STG.u[k]])
            dma(q, dst, STG.t[0:rows, k, 0:nch * 128], STG.u[k], reads=[STG.u[k]], is_out=is_out)

        def ffn(L, n):
            rms_fm(X, 8, lambda c: VECT.t[:, c, 4 + L:5 + L], H, 0, n, D)
            for jg in range(6):
                c0, c1 = jg * 512, min(FH, jg * 512 + 512)
                Wg, bg = ws.get(*wcols(w_gate[L], c0, c1, 8))
                Wu, bu = ws.get(*wcols(w_up[L], c0, c1, 8))
                for m in range((c1 - c0) // 128):
                    j = jg * 4 + m
                    pg = psum()
                    mm(pg, slice(0, 128), slice(0, n), [(Wg[:, k, m * 128:(m + 1) * 128], H.t[:, k, 0:n]) for k in range(8)], H.u + [bg])
                    pu = psum()
                    mm(pu, slice(0, 128), slice(0, n), [(Wu[:, k, m * 128:(m + 1) * 128], H.t[:, k, 0:n]) for k in range(8)], H.u + [bu])
                    k2 = j % 2
                    act(SG.t[:, k2, 0:n], pg.t[:, 0:n], AF.Silu, pg.u, [SG.u[k2]])
                    tt(A.t[:, j, 0:n], SG.t[:, k2, 0:n], pu.t[:, 0:n], ALU.mult, [SG.u[k2]] + pu.u, [A.u[j]])
            for m in range(8):
                Wd, bd = ws.get(w_down[L][:, m * 128:(m + 1) * 128].rearrange("(j p) c -> p j c", p=128), (128, NJ, 128))
                ps = psum()
                mm(ps, slice(0, 128), slice(0, n), [(Wd[:, j, :], A.t[:, j, 0:n]) for j in range(NJ)], A.u + [bd])
                tt(X.t[:, m, 0:n], X.t[:, m, 0:n], ps.t[:, 0:n], ALU.add, [X.u[m]] + ps.u, [X.u[m]])

        def even_layer(e, L, n, sample, tile_idx):
            nt = max(1, n // 128)
            pt = min(n, 128)
            U = lambda c: A.t[:, c, 0:n]
            AO = lambda c: A.t[:, 8 + c, 0:n]
            rms_fm(X, 8, lambda c: VECT.t[:, c, L:L + 1], H, 0, n, D)
            dma("sp", GLN.t[:, 0, :], gmg[e:e + 1, :].partition_broadcast(128), GLN.u[0], writes=[GLN.u[0]])
            dma("sp", GLN.t[:, 1, :], gmb[e:e + 1, :].partition_broadcast(128), GLN.u[1], writes=[GLN.u[1]])
            dma("sp", BSP.t[:, :], bsp[0:1, e * 512:(e + 1) * 512], BSP.u[0], writes=BSP.u)
            for g4 in range(2):
                W, bw = ws.get(*wcols(w_in_ab[e], g4 * 512, g4 * 512 + 512, 8))
                for m in range(4):
                    c = g4 * 4 + m
                    ps = psum()
                    mm(ps, slice(0, 128), slice(0, n), [(W[:, k, m * 128:(m + 1) * 128], H.t[:, k, 0:n]) for k in range(8)], H.u + [bw])
                    gelu_to(U(c), ps, slice(0, 128), n, [A.u[c]])
            W0, b0 = ws.get(*wcols(w_in_ab[e], 1024, 1536, 8))
            W1, b1 = ws.get(*wcols(w_in_ab[e], 1536, 2048, 8))
            if True:
                for q in range(nt):
                    gk = q % 2
                    for half, (W, bw) in enumerate(((W0, b0), (W1, b1))):
                        ps = psum()
                        mm(ps, slice(0, pt), slice(0, 512), [(H.t[:, k, q * 128:q * 128 + pt], W[:, k, :]) for k in range(8)], H.u + [bw])
                        gelu_to(G.t[0:pt, gk, half * 512:(half + 1) * 512], ps, slice(0, pt), 512, [G.u[gk]])
                    g = G.t[0:pt, gk, :]
                    st = SM.t[0:pt, 0, :]
                    P.op("dve", lambda h, g=g: h.bn_stats(out=SM.t[0:pt, 0, 0:6], in_=g[:, 0:512]), reads=[G.u[gk]], writes=[SM.u[0]])
                    P.op("dve", lambda h, g=g: h.bn_stats(out=SM.t[0:pt, 1, 0:6], in_=g[:, 512:1024]), reads=[G.u[gk]], writes=[SM.u[1]])
                    P.op("dve", lambda h: h.bn_aggr(out=SM.t[0:pt, 2, 0:2], in_=SM.t[0:pt, 0:2, 0:6]), reads=[SM.u[0], SM.u[1]], writes=[SM.u[2]])
                    act(SM.t[0:pt, 2, 2:3], SM.t[0:pt, 2, 1:2], AF.Sqrt, [SM.u[2]], [SM.u[2]], bias=EPS)
                    recip(SM.t[0:pt, 2, 3:4], SM.t[0:pt, 2, 2:3], [SM.u[2]], [SM.u[2]])
                    ts(g, g, SM.t[0:pt, 2, 0:1], SM.t[0:pt, 2, 3:4], ALU.subtract, ALU.mult, [G.u[gk], SM.u[2]], [G.u[gk]])
                    tt(g, g, GLN.t[0:pt, 0, :], ALU.mult, [G.u[gk], GLN.u[0]], [G.u[gk]])
                    tt(g, g, GLN.t[0:pt, 1, :], ALU.add, [G.u[gk], GLN.u[1]], [G.u[gk]])
                    vq = BO.t[0:pt, 2 * q:2 * q + 2, :].rearrange("p a b -> p (a b)")
                    copy(vq, g, [G.u[gk]], [BO.u[2 * q], BO.u[2 * q + 1]], eng="act")
                    if sample:
                        dma("sp", o_v_s[e], g, G.u[gk], reads=[G.u[gk]], is_out=True)
                for c in range(8):
                    gi = c // 2
                    ps = psum()
                    for q in range(nt):
                        vq = BO.t[0:pt, 2 * q:2 * q + 2, :].rearrange("p a b -> p (a b)")
                        lhs = vq[:, c * 128:(c + 1) * 128]
                        if sample:
                            mm(ps, slice(0, 128), slice(0, n), [(lhs, W00I.t[0:pt, e * 4 + gi, 0:pt])],
                               [BO.u[2 * q], BO.u[2 * q + 1]] + W00I.u)
                        else:
                            def fn(h, ps=ps, q=q, lhs=lhs, gi=gi):
                                h.matmul(ps.t[:, q * 128:(q + 1) * 128], lhs, WST.t[:, e * 4 + gi, :], start=True, stop=False)
                                return h.matmul(ps.t[:, q * 128:(q + 1) * 128], ONE1.t[0:1, :],
                                                BSP.t[0:1, gi * 128:gi * 128 + 128], start=False, stop=True)
                            P.op("pe", fn, reads=[BO.u[2 * q], BO.u[2 * q + 1]] + WST.u + BSP.u + ONE1.u, writes=ps.u)
                    if sample:
                        stt(AO(c), ps.t[:, 0:n], BC8.t[:, 1, e * 4 + gi:e * 4 + gi + 1], U(c), ALU.add, ALU.mult,
                            ps.u + [A.u[c]] + BC8.u, [A.u[8 + c]])
                    else:
                        tt(AO(c), ps.t[:, 0:n], U(c), ALU.mult, ps.u + [A.u[c]], [A.u[8 + c]])
            if sample:
                rows = NS * 30
                r0 = 0
                while r0 < rows:
                    rr = min(120, rows - r0)
                    gk = (r0 // 120) % 2
                    dma("sp", G.t[0:rr, gk, :], stc[e, r0:r0 + rr, :], G.u[gk], writes=[G.u[gk]])
                    for half in range(2):
                        ps = psum()
                        for i in range(4):
                            c = half * 4 + i
                            mm(ps, slice(0, 128), slice(i * 128, i * 128 + rr), [(G.t[0:rr, gk, c * 128:(c + 1) * 128], IDENT.t[0:rr, 0:rr])],
                               [G.u[gk]] + IDENT.u)
                        for i in range(4):
                            c = half * 4 + i
                            copy(ACC.t[:, c, 32 + r0:32 + r0 + rr], ps.t[:, i * 128:i * 128 + rr], ps.u, [ACC.u[c]])
                    r0 += rr
                dmy = DMY
                for b in range(NS):
                    dma("sp", o_conv_s[e, b * 30:b * 30 + 29, :], stc[e, b * 30 + 1:b * 30 + 30, :], dmy, is_out=True)
            for g4 in range(2):
                Wb, bb = ws.get(*wcols(w_in_ab[e], 2048 + g4 * 512, 2048 + g4 * 512 + 512, 8))
                Wg, bg = ws.get(*wcols(w_in_ab[e], 3072 + g4 * 512, 3072 + g4 * 512 + 512, 8))
                for m in range(4):
                    c = g4 * 4 + m
                    k2 = c % 2
                    pb = psum()
                    mm(pb, slice(0, 128), slice(0, n), [(Wb[:, k, m * 128:(m + 1) * 128], H.t[:, k, 0:n]) for k in range(8)], H.u + [bb])
                    pg = psum()
                    mm(pg, slice(0, 128), slice(0, n), [(Wg[:, k, m * 128:(m + 1) * 128], H.t[:, k, 0:n]) for k in range(8)], H.u + [bg])
                    act(SG.t[:, k2, 0:n], pg.t[:, 0:n], AF.Sigmoid, pg.u, [SG.u[k2]])
                    cw = lambda k, c=c: VECT.t[:, c, 15 + e * 31 + k:16 + e * 31 + k]
                    cb = VECT.t[:, c, 9 + e:10 + e]
                    acc = ACC.t[:, c, 0:n]
                    if sample:
                        glu = XE.t[:, k2, 0:n]
                        tt(glu, SG.t[:, k2, 0:n], pb.t[:, 0:n], ALU.mult, [SG.u[k2]] + pb.u, [XE.u[k2]])
                        st3 = ACC.t[:, c, 32:32 + NS * 30].rearrange("p (b k) -> p b k", k=30)
                        wv = VECT.t[:, c, 15 + e * 31:15 + e * 31 + 30]
                        tmp = T1.t[:, k2, 0:NS * 30].rearrange("p (b k) -> p b k", k=30)
                        P.op("dve", lambda h, tmp=tmp, st3=st3, wv=wv: h.tensor_tensor(
                            out=tmp, in0=st3, in1=wv.unsqueeze(1).broadcast_to([128, NS, 30]), op=ALU.mult),
                            reads=[ACC.u[c]] + VECT.u, writes=[T1.u[k2]])
                        P.op("dve", lambda h, tmp=tmp, k2=k2: h.tensor_reduce(
                            out=STAT.t[:, 1 + k2, 0:NS], in_=tmp, axis=mybir.AxisListType.X, op=ALU.add),
                            reads=[T1.u[k2]], writes=[STAT.u[1 + k2]])
                        stt(acc, glu, cw(30), STAT.t[:, 1 + k2, 0:NS], ALU.mult, ALU.add, [XE.u[k2], STAT.u[1 + k2]] + VECT.u, [ACC.u[c]])
                        ts(acc, acc, cb, None, ALU.add, None, [ACC.u[c]] + VECT.u, [ACC.u[c]])
                        copy(CARRY.t[:, e, c * 30:c * 30 + n], glu, [XE.u[k2]], [CARRY.u[e]], eng="act")
                    else:
                        xe = XEB.t[:, k2, :]
                        copy(xe[:, 0:30], CARRY.t[:, e, c * 30:(c + 1) * 30], [CARRY.u[e]], [XEB.u[k2]], eng="act")
                        tt(xe[:, 30:30 + n], SG.t[:, k2, 0:n], pb.t[:, 0:n], ALU.mult, [SG.u[k2]] + pb.u, [XEB.u[k2]])
                        tt(CARRY.t[:, e, c * 30:(c + 1) * 30], SG.t[:, k2, n - 30:n], pb.t[:, n - 30:n], ALU.mult, [SG.u[k2]] + pb.u, [CARRY.u[e]])
                        pc = psum()
                        for u in range(2):
                            taps = list(range(16)) if u == 0 else list(range(16, 31))
                            nt_ = len(taps)
                            wv_ = VECT.t[:, c, 15 + e * 31 + taps[0]:15 + e * 31 + taps[0] + nt_]
                            dgv = DG.t[:, u, 0:nt_ * 128].rearrange("p (k j) -> p k j", j=128)
                            P.op("dve", lambda h_, dgv=dgv, wv_=wv_, nt_=nt_: h_.tensor_tensor(
                                out=dgv, in0=IDENTB.t[:, :].unsqueeze(1).broadcast_to([128, nt_, 128]),
                                in1=wv_.unsqueeze(2).broadcast_to([128, nt_, 128]), op=ALU.mult),
                                reads=IDENTB.u + VECT.u, writes=[DG.u[u]])
                            mm(pc, slice(0, 128), slice(0, n), [(DG.t[:, u, kk * 128:(kk + 1) * 128], xe[:, k:k + n]) for kk, k in enumerate(taps)],
                               [DG.u[u], XEB.u[k2]], start=(u == 0), stop=(u == 1))
                        act(acc, pc.t[:, 0:n], AF.Identity, pc.u + VECT.u, [ACC.u[c]], bias=cb)
            if True:
                p1 = psum()
                p2 = psum()
                for c in range(8):
                    k = c % 2
                    copy(SQ.t[:, k, 0:n], ACC.t[:, c, 0:n], [ACC.u[c]], [SQ.u[k]], eng="act")
                    mm(p1, slice(0, 128), slice(0, n), [(ONESB.t[:, :], SQ.t[:, k, 0:n])], [SQ.u[k]] + ONESB.u, start=(c == 0), stop=(c == 7))
                    act(SQ.t[:, k, 0:n], ACC.t[:, c, 0:n], AF.Square, [ACC.u[c]], [SQ.u[k]])
                    mm(p2, slice(0, 128), slice(0, n), [(ONESB.t[:, :], SQ.t[:, k, 0:n])], [SQ.u[k]] + ONESB.u, start=(c == 0), stop=(c == 7))
                mu = STAT.t[:, 1, 0:n]
                rs = STAT.t[:, 2, 0:n]
                ts(mu, p1.t[:, 0:n], 1.0 / D, None, ALU.mult, None, p1.u, [STAT.u[1]])
                tt(rs, mu, mu, ALU.mult, [STAT.u[1]], [STAT.u[2]])
                stt(rs, p2.t[:, 0:n], 1.0 / D, rs, ALU.mult, ALU.subtract, p2.u + [STAT.u[2]], [STAT.u[2]])
                act(rs, rs, AF.Sqrt, [STAT.u[2]], [STAT.u[2]], bias=EPS)
                recip(rs, rs, [STAT.u[2]], [STAT.u[2]])
                for c in range(8):
                    acc = ACC.t[:, c, 0:n]
                    tt(acc, acc, mu, ALU.subtract, [ACC.u[c], STAT.u[1]], [ACC.u[c]])
                    tt(acc, acc, rs, ALU.mult, [ACC.u[c], STAT.u[2]], [ACC.u[c]])
                    act(BO.t[:, c, 0:n], acc, AF.Silu, [ACC.u[c]] + VECT.u, [BO.u[c]],
                        scale=VECT.t[:, c, 11 + e:12 + e], bias=VECT.t[:, c, 13 + e:14 + e])
            for g4 in range(4):
                W, bw = ws.get(*wcols(w_out_ab[e], g4 * 256, g4 * 256 + 256, 16))
                for m in range(2):
                    mc = g4 * 2 + m
                    ps = psum()
                    pairs = [(W[:, k, m * 128:(m + 1) * 128], A.t[:, 8 + k, 0:n]) for k in range(8)]
                    pairs += [(W[:, 8 + k, m * 128:(m + 1) * 128], BO.t[:, k, 0:n]) for k in range(8)]
                    mm(ps, slice(0, 128), slice(0, n), pairs, A.u[8:16] + BO.u + [bw])
                    tt(X.t[:, mc, 0:n], X.t[:, mc, 0:n], ps.t[:, 0:n], ALU.add, [X.u[mc]] + ps.u, [X.u[mc]])
            if True:
                if sample:
                    fm_to_tm(lambda c: CARRY.t[:, e, c * 30:c * 30 + n], lambda c: [CARRY.u[e]], 8, n,
                             o_conv_s[e].rearrange("(b k) d -> b k d", k=30)[:, 29, :])
                elif tile_idx == NTILE - 1:
                    fm_to_tm(lambda c: CARRY.t[:, e, c * 30:(c + 1) * 30], lambda c: [CARRY.u[e]], 8, 30, o_conv_p[e])

        def odd_layer(o, L, n, sample, tile_idx):
            nt = max(1, n // 128)
            pt = min(n, 128)
            t0 = tile_idx * TT
            QN = lambda h: A.t[:, h, 0:n]
            OH = lambda h: A.t[:, 8 + h, 0:n]
            CQN = lambda c: A.t[:, 16 + c, 0:n]
            CKT = lambda c: A.t[:, 20 + c, 0:n]
            rms_fm(X, 8, lambda c: VECT.t[:, c, L:L + 1], H, 0, n, D)
            Wc, bc = ws.get(*wcols(w_in_c[o], 0, 512, 8))
            if True:
                if sample:
                    dma("sp", CST.t[0:pt, 0, :], c_cs_s.partition_broadcast(pt), CST.u[0], writes=CST.u)
                    dma("sp", CSF.t[:, 0, 0:n], c_cs_fs[0], CSF.u[0], writes=[CSF.u[0]])
                    dma("sp", CSF.t[:, 1, 0:n], c_cs_fs[1], CSF.u[1], writes=[CSF.u[1]])
                else:
                    dma("sp", CST.t[:, :, :], c_cs_tm[t0:t0 + n, :].rearrange("(q p) f -> p q f", p=128), CST.u[0], writes=CST.u)
                    dma("sp", CSF.t[:, 0, 0:n], c_cs_fm[0, :, t0:t0 + n], CSF.u[0], writes=[CSF.u[0]])
                    dma("sp", CSF.t[:, 1, 0:n], c_cs_fm[1, :, t0:t0 + n], CSF.u[1], writes=[CSF.u[1]])
                for m in range(4):
                    ps = psum()
                    mm(ps, slice(0, 128), slice(0, n), [(Wc[:, k, m * 128:(m + 1) * 128], H.t[:, k, 0:n]) for k in range(8)], H.u + [bc])
                    copy(ACC.t[:, m, 0:n], ps.t[:, 0:n], ps.u, [ACC.u[m]])
                rms_fm(ACC, 4, lambda c: VECT.t[:, 4 * o + c, 77:78], A, 16, n, 512)
                Wk, bk = ws.get(*wcols(w_in_c[o], 512, 832, 8))
                for q in range(nt):
                    ps = psum()
                    mm(ps, slice(0, pt), slice(0, 320), [(H.t[:, k, q * 128:q * 128 + pt], Wk[:, k, :]) for k in range(8)], H.u + [bk])
                    tok = TOK.t[0:pt, q, :]
                    sm = SM.t[0:pt, 3, :]
                    act(T1.t[0:pt, 0, 0:256], ps.t[0:pt, 0:256], AF.Square, ps.u, [T1.u[0], SM.u[3]], accum=SM.t[0:pt, 3, 0:1])
                    act(SM.t[0:pt, 3, 1:2], SM.t[0:pt, 3, 0:1], AF.Sqrt, [SM.u[3]], [SM.u[3]], scale=1.0 / 256, bias=EPS)
                    recip(SM.t[0:pt, 3, 2:3], SM.t[0:pt, 3, 1:2], [SM.u[3]], [SM.u[3]])
                    stt(tok[:, 0:256], ps.t[0:pt, 0:256], SM.t[0:pt, 3, 2:3], KVN.t[0:pt, o, :], ALU.mult, ALU.mult,
                        ps.u + [SM.u[3]] + KVN.u, [TOK.u[q]])
                    x1 = ps.t[0:pt, 256:288]
                    x2 = ps.t[0:pt, 288:320]
                    cq_ = 0 if sample else q
                    cs_ = CST.t[0:pt, cq_, 0:32]
                    sn_ = CST.t[0:pt, cq_, 32:64]
                    sc = RSC.t[0:pt, :]
                    smu = RSC.u
                    tt(sc[:, 0:32], x1, cs_, ALU.mult, ps.u + CST.u + [SM.u[3]], smu)
                    tt(sc[:, 32:64], x2, sn_, ALU.mult, ps.u + CST.u, smu)
                    tt(tok[:, 256:288], sc[:, 0:32], sc[:, 32:64], ALU.subtract, smu, [TOK.u[q]])
                    tt(sc[:, 0:32], x1, sn_, ALU.mult, ps.u + CST.u + [TOK.u[q]], smu)
                    tt(sc[:, 32:64], x2, cs_, ALU.mult, ps.u + CST.u, smu)
                    tt(tok[:, 288:320], sc[:, 0:32], sc[:, 32:64], ALU.add, smu, [TOK.u[q]])
                    if sample:
                        dma("sp", o_ckv_s[o], tok[:, 0:256], TOK.u[q], reads=[TOK.u[q]], is_out=True)
                        dma("sp", o_kpe_s[o], tok[:, 256:320], TOK.u[q], reads=[TOK.u[q]], is_out=True)
                    else:
                        r0 = t0 + q * 128
                        dma("sp", o_ckv_p[o, r0:r0 + pt, :], tok[:, 0:256], TOK.u[q], reads=[TOK.u[q]], is_out=True)
                        dma("sp", o_kpe_p[o, r0:r0 + pt, :], tok[:, 256:320], TOK.u[q], reads=[TOK.u[q]], is_out=True)
                    for cc in range(2):
                        p2 = psum()
                        mm(p2, slice(0, 128), slice(0, pt), [(tok[:, cc * 128:(cc + 1) * 128], IDENT.t[0:pt, 0:pt])], [TOK.u[q]] + IDENT.u)
                        copy(A.t[:, 20 + cc, q * 128:q * 128 + pt], p2.t[:, 0:pt], p2.u, [A.u[20 + cc]])
                    p3 = psum()
                    mm(p3, slice(0, 64), slice(0, pt), [(tok[:, 256:320], IDENT.t[0:pt, 0:pt])], [TOK.u[q]] + IDENT.u)
                    copy(SQ.t[0:64, 0, q * 128:q * 128 + pt], p3.t[0:64, 0:pt], p3.u, [SQ.u[0]])
                if not sample:
                    Wuk, buk = ws.get(w_uk[o].rearrange("(k p) c -> p k c", p=128), (128, 2, 1024))
                    for h in range(8):
                        ps = psum()
                        mm(ps, slice(0, 128), slice(0, n), [(Wuk[:, cc, h * 128:(h + 1) * 128], CKT(cc)) for cc in range(2)], [A.u[20], A.u[21], buk])
                        copy(BO.t[:, h, 0:n], ps.t[:, 0:n], ps.u, [BO.u[h]])
                        dma("sp", Kn_d[o, h, :, t0:t0 + n], BO.t[:, h, 0:n], BO.u[h], reads=[BO.u[h]], dappend=[dKn[o]])
                    dma("sp", Kp_d[o, :, t0:t0 + n], SQ.t[0:64, 0, 0:n], SQ.u[0], reads=[SQ.u[0]], dappend=[dKp[o]])
                    Wuv, buv = ws.get(w_uv[o].rearrange("(k p) c -> p k c", p=128), (128, 2, 1024))
                    for q in range(nt):
                        for half in range(2):
                            ps = psum()
                            mm(ps, slice(0, pt), slice(0, 512), [(A.t[:, 20 + cc, q * 128:q * 128 + pt], Wuv[:, cc, half * 512:(half + 1) * 512]) for cc in range(2)],
                               [A.u[20], A.u[21], buv])
                            copy(VST.t[0:pt, q % 2, half * 512:(half + 1) * 512], ps.t[0:pt, 0:512], ps.u, [VST.u[q % 2]])
                        dma("sp", V_d[o, t0 + q * 128:t0 + q * 128 + pt, :], VST.t[0:pt, q % 2, :], VST.u[q % 2],
                            reads=[VST.u[q % 2]], dappend=[dV[o]])
            for hg in range(2):
                Wq, bq = ws.get(*wcols(w_uq[o], hg * 768, hg * 768 + 768, 4))
                for hh in range(4):
                    src1 = Wq[:, :, hh * 192 + 160:hh * 192 + 192]
                    src2 = Wq[:, :, hh * 192 + 128:hh * 192 + 160]
                    P.op("dve", lambda h_, hh=hh, src1=src1: h_.tensor_scalar(out=WR.t[:, :, hh * 64:hh * 64 + 32], in0=src1, scalar1=-1.0, scalar2=None, op0=ALU.mult),
                         reads=[bq], writes=WR.u)
                    P.op("dve", lambda h_, hh=hh, src2=src2: h_.tensor_copy(out=WR.t[:, :, hh * 64 + 32:hh * 64 + 64], in_=src2), reads=[bq], writes=WR.u)
                for hh in range(4):
                    h = hg * 4 + hh
                    cqr = [A.u[16 + k] for k in range(4)]
                    ps = psum()
                    mm(ps, slice(0, 128), slice(0, n), [(Wq[:, k, hh * 192:hh * 192 + 128], CQN(k)) for k in range(4)], cqr + [bq])
                    copy(QN(h), ps.t[:, 0:n], ps.u, [A.u[h]])
                    pa = psum()
                    mm(pa, slice(0, 64), slice(0, n), [(Wq[:, k, hh * 192 + 128:hh * 192 + 192], CQN(k)) for k in range(4)], cqr + [bq])
                    pb = psum()
                    mm(pb, slice(0, 64), slice(0, n), [(WR.t[:, k, hh * 64:(hh + 1) * 64], CQN(k)) for k in range(4)], cqr + WR.u)
                    ta = T1.t[0:64, 0, 0:n]
                    tb = T1.t[0:64, 1, 0:n]
                    tt(ta, pa.t[0:64, 0:n], CSF.t[:, 0, 0:n], ALU.mult, pa.u + [CSF.u[0]], [T1.u[0]])
                    tt(tb, pb.t[0:64, 0:n], CSF.t[:, 1, 0:n], ALU.mult, pb.u + [CSF.u[1]], [T1.u[1]])
                    tt(H.t[0:64, h, 0:n], ta, tb, ALU.add, [T1.u[0], T1.u[1]], [H.u[h]])
            if not sample:
                nblk = 4 * tile_idx + 4
                nseg = (nblk * 128 + KSEG - 1) // KSEG
                bps = KSEG // 128
                items = []
                for h in range(8):
                    for sg in range(nseg):
                        nb = (min(nblk * 128, sg * KSEG + KSEG) - sg * KSEG) // 128
                        for j in range(nb):
                            items.append((h, sg, j, sg * bps + j))

                def loads(it):
                    h, sg, j, kb = it
                    sk = (h * nseg + sg) % 2
                    k0 = sg * KSEG
                    k1 = min(nblk * 128, k0 + KSEG)
                    nb = (k1 - k0) // 128
                    dma("sp", KN.t[:, sk, 0:k1 - k0], Kn_d[o, h, :, k0:k1], KN.u[sk], writes=[KN.u[sk]], dreads=[dKn[o]])
                    dma("sp", KP.t[0:64, sk, 0:k1 - k0], Kp_d[o, :, k0:k1], KP.u[sk], writes=[KP.u[sk]], dreads=[dKp[o]])
                    dma("sp", VV.t[:, sk, 0:nb * 128].rearrange("p (j v) -> p j v", v=128),
                        V_d[o, k0:k1, h * 128:(h + 1) * 128].rearrange("(j p) v -> p j v", p=128), VV.u[sk], writes=[VV.u[sk]], dreads=[dV[o]])

                def qk(it):
                    h, sg, j, kb = it
                    sk = (h * nseg + sg) % 2
                    d = kb - 4 * tile_idx
                    n0 = 128 * d if d > 0 else 0
                    pS = psum()
                    mm(pS, slice(0, 128), slice(n0, n), [(KN.t[:, sk, j * 128:(j + 1) * 128], A.t[:, h, n0:n]),
                                                       (KP.t[0:64, sk, j * 128:(j + 1) * 128], H.t[0:64, h, n0:n])],
                       [KN.u[sk], KP.u[sk], A.u[h], H.u[h]])
                    return pS

                def rest(it, pS):
                    h, sg, j, kb = it
                    sk = (h * nseg + sg) % 2
                    pO = PS[h % 2]
                    pL = PS[2 + h % 2]
                    d = kb - 4 * tile_idx
                    n0 = 128 * d if d > 0 else 0
                    pk = kb % 2
                    pt_ = PTb.t[:, pk, n0:n]
                    act(pt_, pS.t[:, n0:n], AF.Exp, pS.u, [PTb.u[pk]], scale=SCALE)
                    if d >= 0:
                        tt(PTb.t[:, pk, n0:n0 + 128], PTb.t[:, pk, n0:n0 + 128], TRIB.t[:, :], ALU.mult, [PTb.u[pk]] + TRIB.u, [PTb.u[pk]])
                    first = (kb == 0)
                    last = (kb == nblk - 1)
                    mm(pO, slice(0, 128), slice(n0, n), [(VV.t[:, sk, j * 128:(j + 1) * 128], pt_)], [VV.u[sk], PTb.u[pk]], start=first, stop=last)
                    mm(pL, slice(0, 128), slice(n0, n), [(ONESB.t[:, :], pt_)], [PTb.u[pk]] + ONESB.u, start=first, stop=last)
                    if last:
                        rl = STAT.t[:, 1 + h % 2, 0:n]
                        recip(rl, pL.t[:, 0:n], pL.u, [STAT.u[1 + h % 2]])
                        tt(OH(h), pO.t[:, 0:n], rl, ALU.mult, pO.u + [STAT.u[1 + h % 2]], [A.u[8 + h]])

                loads(items[0])
                pS_next = qk(items[0])
                for i, it in enumerate(items):
                    pS_cur = pS_next
                    if i + 1 < len(items):
                        nx = items[i + 1]
                        if nx[2] == 0:
                            loads(nx)
                        pS_next = qk(nx)
                    rest(it, pS_cur)
            else:
                sample_attn(o, n)
            for g4 in range(2):
                W, bw = ws.get(*wcols(w_out_c[o], g4 * 512, g4 * 512 + 512, 8))
                for m in range(4):
                    mc = g4 * 4 + m
                    ps = psum()
                    mm(ps, slice(0, 128), slice(0, n), [(W[:, k, m * 128:(m + 1) * 128], A.t[:, 8 + k, 0:n]) for k in range(8)], A.u[8:16] + [bw])
                    tt(X.t[:, mc, 0:n], X.t[:, mc, 0:n], ps.t[:, 0:n], ALU.add, [X.u[mc]] + ps.u, [X.u[mc]])

        def sample_attn(o, n):
            Wuk, buk = ws.get(w_uk[o].rearrange("(k p) c -> p k c", p=128), (128, 2, 1024))
            for cc in range(2):
                for h in range(8):
                    ps = psum()
                    mm(ps, slice(0, 128), slice(0, 128), [(Wuk[:, cc, h * 128:(h + 1) * 128], IDENTB.t[:, :])], [buk] + IDENTB.u)
                    copy(WUKT.t[:, h, cc * 128:(cc + 1) * 128], ps.t[:, 0:128], ps.u, WUKT.u)
            ql = QL.t[:, :].rearrange("p (c b h) -> p c b h", c=2, b=NS)
            for h in range(8):
                for cc in range(2):
                    ps = psum()
                    mm(ps, slice(0, 128), slice(0, n), [(WUKT.t[:, h, cc * 128:(cc + 1) * 128], A.t[:, h, 0:n])], WUKT.u + [A.u[h]])
                    copy(ql[:, cc, :, h], ps.t[:, 0:n], ps.u, QL.u)
            qps = QPS.t[:, :].rearrange("p (b h) -> p b h", h=8)
            for h in range(8):
                copy(qps[:, :, h], H.t[0:64, h, 0:n], [H.u[h]], QPS.u, eng="dve")
            olt = OLT.t[:, :].rearrange("p (c h b) -> p c h b", c=2, h=8)
            ngrp = NPG // GP
            items = [(b, pg) for b in range(NS) for pg in range(ngrp)]
            NI = len(items)

            def g_dma(s_):
                b, pg = items[s_]
                nk = b % 2
                gk = s_ % NCK
                if pg == 0:
                    ix = IDX.t[:, nk, :]
                    dma("sp", ix, ptab[0:1, b * NPG:(b + 1) * NPG].partition_broadcast(128), IDX.u[nk], writes=[IDX.u[nk]])
                    ts(ix, ix, 128.0, float(o * NPOOL * 128), ALU.mult, ALU.add, [IDX.u[nk]], [IDX.u[nk]])
                    ts(ix, ix, IOTAF.t[:, 0:1], None, ALU.add, None, [IDX.u[nk]] + IOTAF.u, [IDX.u[nk]])
                for i in range(GP):
                    col = pg * GP + i
                    P.dma("pool", lambda h_, gk=gk, i=i, col=col, nk=nk: h_.indirect_dma_start(
                        out=CK.t[:, gk, i * 324:i * 324 + 320], out_offset=None, in_=cpool[:, :],
                        in_offset=bass.IndirectOffsetOnAxis(ap=IDX.t[:, nk, col:col + 1], axis=0)),
                        CK.u[gk], reads=[IDX.u[nk]], writes=[CK.u[gk]])

            def g_T(s_):
                gk = s_ % NCK
                tk = s_ % 3
                for part in range(3):
                    ps = psum()
                    rows = 64 if part == 2 else 128
                    for i in range(GP):
                        src = CK.t[:, gk, i * 324 + part * 128:i * 324 + part * 128 + rows]
                        mm(ps, slice(0, rows), slice(i * 128, (i + 1) * 128), [(src, IDENTB.t[:, :])], [CK.u[gk]] + IDENTB.u)
                    copy(KT.t[0:rows, tk, part * GP * 128:(part + 1) * GP * 128], ps.t[0:rows, 0:GP * 128], ps.u, [KT.u[tk]],
                         eng=("act" if part != 1 else "dve"))

            def g_S(s_):
                b, pg = items[s_]
                tk = s_ % 3
                pk = s_ % 2
                pS = psum()
                for i in range(GP):
                    mm(pS, slice(0, 128), slice(i * 8, (i + 1) * 8),
                       [(KT.t[:, tk, i * 128:(i + 1) * 128], ql[:, 0, b, :]),
                        (KT.t[:, tk, GP * 128 + i * 128:GP * 128 + (i + 1) * 128], ql[:, 1, b, :]),
                        (KT.t[0:64, tk, 2 * GP * 128 + i * 128:2 * GP * 128 + (i + 1) * 128], qps[:, b, :])],
                       [KT.u[tk]] + QL.u + QPS.u)
                act(PSS.t[:, pk, 0:GP * 8], pS.t[:, 0:GP * 8], AF.Exp, pS.u, [PSS.u[pk]], scale=SCALE)

            def g_PV(s_):
                b, pg = items[s_]
                gk = s_ % NCK
                pk = s_ % 2
                nk = b % 2
                pO = PS[b % 2]
                pLs = PS[2 + b % 2]
                for i in range(GP):
                    st_ = (pg == 0 and i == 0)
                    mm(pO, slice(0, 8), slice(0, 256), [(PSS.t[:, pk, i * 8:(i + 1) * 8], CK.t[:, gk, i * 324:i * 324 + 256])],
                       [PSS.u[pk], CK.u[gk]], start=st_, stop=False)
                    mm(pLs, slice(0, 8), slice(0, 1), [(PSS.t[:, pk, i * 8:(i + 1) * 8], ONEC.t[:, 0:1])],
                       [PSS.u[pk]] + ONEC.u, start=st_, stop=False)
                if pg != ngrp - 1:
                    return
                dma("sp", NEWROW.t[0:1, nk, 0:256], TOK.t[b:b + 1, 0, 0:256], NEWROW.u[nk], reads=[TOK.u[0]], writes=[NEWROW.u[nk]])
                p1 = psum()
                mm(p1, slice(0, 1), slice(0, 8), [(A.t[:, 20, b:b + 1], ql[:, 0, b, :]), (A.t[:, 21, b:b + 1], ql[:, 1, b, :]),
                                                 (SQ.t[0:64, 0, b:b + 1], qps[:, b, :])], [A.u[20], A.u[21], SQ.u[0]] + QL.u + QPS.u)
                act(PN.t[0:1, nk, :], p1.t[0:1, 0:8], AF.Exp, p1.u, [PN.u[nk]], scale=SCALE)
                P.op("dve", lambda h_, nk=nk: h_.memset(NEWROW.t[0:1, nk, 256:257], 1.0), writes=[NEWROW.u[nk]])
                mm(pO, slice(0, 8), slice(0, 256), [(PN.t[0:1, nk, :], NEWROW.t[0:1, nk, 0:256])], [PN.u[nk], NEWROW.u[nk]], start=False, stop=True)
                mm(pLs, slice(0, 8), slice(0, 1), [(PN.t[0:1, nk, :], NEWROW.t[0:1, nk, 256:257])], [PN.u[nk], NEWROW.u[nk]], start=False, stop=True)
                recip(SM.t[0:8, nk, 0:1], pLs.t[0:8, 0:1], pLs.u, [SM.u[nk]])
                ts(OL.t[0:8, nk, :], pO.t[0:8, 0:256], SM.t[0:8, nk, 0:1], None, ALU.mult, None, pO.u + [SM.u[nk]], [OL.u[nk]])
                for cc in range(2):
                    ps = psum()
                    mm(ps, slice(0, 128), slice(0, 8), [(OL.t[0:8, nk, cc * 128:(cc + 1) * 128], IDENT.t[0:8, 0:8])], [OL.u[nk]] + IDENT.u)
                    copy(olt[:, cc, :, b], ps.t[:, 0:8], ps.u, OLT.u)

            for s_ in range(min(3, NI)):
                g_dma(s_)
            for s_ in range(NI + 2):
                if s_ < NI:
                    g_T(s_)
                if 1 <= s_ <= NI:
                    g_S(s_ - 1)
                if s_ >= 2:
                    g_PV(s_ - 2)
                if s_ + 3 < NI:
                    g_dma(s_ + 3)
            Wuv, buv = ws.get(w_uv[o].rearrange("(k p) c -> p k c", p=128), (128, 2, 1024))
            for h in range(8):
                ps = psum()
                mm(ps, slice(0, 128), slice(0, n), [(Wuv[:, cc, h * 128:(h + 1) * 128], olt[:, cc, h, :]) for cc in range(2)], OLT.u + [buv])
                copy(A.t[:, 8 + h, 0:n], ps.t[:, 0:n], ps.u, [A.u[8 + h]])

        def load_x(src, n):
            nt = max(1, n // 128)
            pt = min(n, 128)
            for q in range(nt):
                k = q % 2
                dma("sp", STG.t[0:pt, k, :], src[q * 128:q * 128 + pt, :], STG.u[k], writes=[STG.u[k]])
                for half in range(2):
                    ps = psum()
                    for i in range(4):
                        c = half * 4 + i
                        mm(ps, slice(0, 128), slice(i * 128, i * 128 + pt), [(STG.t[0:pt, k, c * 128:(c + 1) * 128], IDENT.t[0:pt, 0:pt])], [STG.u[k]] + IDENT.u)
                    for i in range(4):
                        c = half * 4 + i
                        copy(X.t[:, c, q * 128:q * 128 + pt], ps.t[:, i * 128:i * 128 + pt], ps.u, [X.u[c]], eng=("act" if half else "dve"))

        def final_norm(dst, n):
            nt = max(1, n // 128)
            pt = min(n, 128)
            ps = psum()
            for c in range(8):
                k = c % 2
                act(SQ.t[:, k, 0:n], X.t[:, c, 0:n], AF.Square, [X.u[c]], [SQ.u[k]])
                mm(ps, slice(0, 128), slice(0, n), [(ONESB.t[:, :], SQ.t[:, k, 0:n])], [SQ.u[k]] + ONESB.u, start=(c == 0), stop=(c == 7))
            rs = STAT.t[:, 0, 0:n]
            act(rs, ps.t[:, 0:n], AF.Sqrt, ps.u, [STAT.u[0]], scale=1.0 / D, bias=EPS)
            recip(rs, rs, [STAT.u[0]], [STAT.u[0]])
            for c in range(8):
                stt(ACC.t[:, c, 0:n], X.t[:, c, 0:n], VECT.t[:, c, 8:9], rs, ALU.mult, ALU.mult, [X.u[c], STAT.u[0]] + VECT.u, [ACC.u[c]])
            for q in range(nt):
                fm_to_tm(lambda c, q=q: ACC.t[:, c, q * 128:q * 128 + pt], lambda c: [ACC.u[c]], 8, pt, dst[q * 128:q * 128 + pt, :])

        import os
        KSET = os.environ.get("KSET", "abcdef")

        def setup():
            dma("sp", IDENT.t[:, :], c_ident, IDENT.u[0], writes=IDENT.u)
            if "a" in KSET:
                dma("sp", TRI.t[:, :], c_tri, TRI.u[0], writes=TRI.u)
            if "b" in KSET:
                for o in range(2):
                    dma("sp", KVN.t[:, o, :], kvn[o:o + 1, :].partition_broadcast(128), KVN.u[0], writes=KVN.u)
                dma("sp", BC8.t[:, 0, :], w00.partition_broadcast(128), BC8.u[0], writes=BC8.u)
                dma("sp", BC8.t[:, 1, :], b00.partition_broadcast(128), BC8.u[0], writes=BC8.u)
            if "c" in KSET:
                copy(IDENTB.t[:, :], IDENT.t[:, :], IDENT.u, IDENTB.u, eng="dve")
                copy(TRIB.t[:, :], TRI.t[:, :], TRI.u, TRIB.u, eng="dve")
                P.op("dve", lambda h: h.memset(ONE1.t[:, :], 1.0), writes=ONE1.u)
                P.op("dve", lambda h: h.memset(CARRY.t[:, :, :], 0.0), writes=CARRY.u)
            P.op("dve", lambda h: h.memset(ONESB.t[:, :], 1.0), writes=ONESB.u)
            P.op("dve", lambda h: h.memset(ONEC.t[:, :], 1.0), writes=ONEC.u)
            if "d" in KSET:
                P.op("pool", lambda h: h.iota(IOTA.t[:, :], [[0, 1]], base=0, channel_multiplier=1), writes=IOTA.u)
                copy(IOTAF.t[:, :], IOTA.t[:, :], IOTA.u, IOTAF.u, eng="dve")
            dma("sp", STG.t[0:78, 0, :], vecs, STG.u[0], writes=[STG.u[0]])
            for half in range(2):
                ps = psum()
                for i in range(4):
                    c = half * 4 + i
                    mm(ps, slice(0, 128), slice(i * 128, i * 128 + 78), [(STG.t[0:78, 0, c * 128:(c + 1) * 128], IDENT.t[0:78, 0:78])], [STG.u[0]] + IDENT.u)
                for i in range(4):
                    c = half * 4 + i
                    copy(VECT.t[:, c, :], ps.t[:, i * 128:i * 128 + 78], ps.u, VECT.u)
            if "f" in KSET:
                for e in range(2):
                    for g in range(4):
                        k = (e * 4 + g) % 2
                        dma("sp", STG.t[:, k, 0:128], wsp[e, g], STG.u[k], writes=[STG.u[k]])
                        ps = psum()
                        mm(ps, slice(0, 128), slice(0, 128), [(STG.t[:, k, 0:128], IDENT.t[:, :])], [STG.u[k]] + IDENT.u)
                        tt(WST.t[:, e * 4 + g, :], ps.t[:, 0:128], TRI.t[:, :], ALU.mult, ps.u + TRI.u, WST.u)
                        ts(W00I.t[:, e * 4 + g, :], IDENT.t[0:16, 0:16], BC8.t[0:16, 0, e * 4 + g:e * 4 + g + 1], None, ALU.mult, None,
                           IDENT.u + BC8.u, W00I.u)

        DBG = os.environ.get("KDBG", "eofsxn")

        def program():
            setup()
            for t in range(NTILE + 1):
                sample = (t == NTILE)
                if sample and "s" not in DBG:
                    continue
                n = NS if sample else TT
                if "x" in DBG:
                    load_x(xs if sample else xp[t * TT:(t + 1) * TT, :], n)
                for L in range(4):
                    if L % 2 == 0:
                        if "e" in DBG:
                            even_layer(L // 2, L, n, sample, t)
                    else:
                        if "o" in DBG:
                            odd_layer(L // 2, L, n, sample, t)
                    if "f" in DBG:
                        ffn(L, n)
                if "n" in DBG:
                    final_norm(y_s if sample else y_p[t * TT:(t + 1) * TT, :], n)

        P.dry = True
        program()
        P.dry = False
        program()
        P.finish()
        with nc.Block() as block:
            P.emit(block)
    return nc


def _consts(SEQ, PAST):
    half = 32
    inv = (10000.0 ** (-np.arange(half, dtype=np.float32) / half)).astype(np.float32)
    pos = np.arange(SEQ, dtype=np.float32)
    ang = pos[:, None] * inv[None, :]
    cs_tm = np.concatenate([np.cos(ang), np.sin(ang)], axis=1).astype(np.float32)
    cs_fm = np.stack([np.concatenate([np.cos(ang).T, np.cos(ang).T], 0),
                      np.concatenate([np.sin(ang).T, np.sin(ang).T], 0)]).astype(np.float32)
    angs = (np.float32(PAST) * inv).astype(np.float32)
    cs_s = np.concatenate([np.cos(angs), np.sin(angs)])[None, :].astype(np.float32)
    cs_fm_s = np.stack([np.concatenate([np.cos(angs), np.cos(angs)]), np.concatenate([np.sin(angs), np.sin(angs)])]).astype(np.float32)
    ident = np.eye(128, dtype=np.float32)
    tri = np.triu(np.ones((128, 128), np.float32))
    return cs_tm, cs_fm, cs_s, cs_fm_s, ident, tri


_NC_CACHE = {}


def kernel(x_prompt, x_sample, cache_ckv, cache_kpe, page_table, state_conv,
           norm_mix, norm_ffn, norm_final,
           w_in_ab, gmlp_ln_g, gmlp_ln_b, w_spatial, b_spatial, conv_w, conv_b, conv_ln_g, conv_ln_b, w_out_ab,
           w_in_c, q_norm, kv_norm, w_uq, w_uk, w_uv, w_out_c,
           w_gate, w_up, w_down, _trace=False):
    f = lambda a: np.ascontiguousarray(np.asarray(a, dtype=np.float32))
    x_prompt, x_sample = f(x_prompt), f(x_sample)
    B, SEQ, _ = x_prompt.shape
    DB = x_sample.shape[0]
    NC = 8
    NS = DB // NC
    page_table = np.ascontiguousarray(np.asarray(page_table, dtype=np.int32))
    NPG = page_table.shape[1]
    NPOOL = cache_ckv.shape[1]
    PAST = NPG * cache_ckv.shape[2]
    key = (SEQ, NS, NPG, NPOOL)
    if key not in _NC_CACHE:
        _NC_CACHE[key] = build_nc(*key)
    nc = _NC_CACHE[key]
    cs_tm, cs_fm, cs_s, cs_fm_s, ident, tri = _consts(SEQ, PAST)
    conv_w, conv_b = f(conv_w), f(conv_b)
    q_norm = f(q_norm)
    vecs = np.concatenate([f(norm_mix), f(norm_ffn), f(norm_final)[None], conv_b, f(conv_ln_g), f(conv_ln_b),
                           conv_w.reshape(2 * 31, D), q_norm.reshape(1, D)], axis=0)
    assert vecs.shape == (78, D)
    w_spatial, b_spatial = f(w_spatial), f(b_spatial)
    shared = dict(
        cpool=np.concatenate([f(cache_ckv).reshape(2 * NPOOL * 128, 256), f(cache_kpe).reshape(2 * NPOOL * 128, 64)], axis=1),
        vecs=vecs, gmg=f(gmlp_ln_g), gmb=f(gmlp_ln_b), kvn=f(kv_norm), wsp=w_spatial,
        bsp=b_spatial.reshape(1, 2 * 512), w00=np.ascontiguousarray(w_spatial[:, :, 0, 0]).reshape(1, 8),
        b00=np.ascontiguousarray(b_spatial[:, :, 0]).reshape(1, 8),
        w_in_ab=f(w_in_ab), w_out_ab=f(w_out_ab), w_in_c=f(w_in_c), w_uq=f(w_uq),
        w_uk=f(w_uk).reshape(2, 256, 1024), w_uv=f(w_uv).reshape(2, 256, 1024), w_out_c=f(w_out_c),
        w_gate=f(w_gate), w_up=f(w_up), w_down=f(w_down),
        c_ident=ident, c_tri=tri, c_cs_tm=cs_tm, c_cs_s=cs_s, c_cs_fm=cs_fm,
        c_cs_fs=np.ascontiguousarray(np.repeat(cs_fm_s[:, :, None], NS, axis=2)),
    )
    state_conv = f(state_conv)
    zeros_x = np.zeros((SEQ, D), np.float32)
    in_maps = []
    for c in range(NC):
        m = dict(shared)
        m["xp"] = x_prompt[c] if c < B else zeros_x
        m["xs"] = x_sample[c * NS:(c + 1) * NS, 0, :]
        m["ptab"] = page_table[c * NS:(c + 1) * NS].reshape(1, NS * NPG)
        m["stc"] = np.ascontiguousarray(state_conv[:, c * NS:(c + 1) * NS]).reshape(2, NS * 30, D)
        in_maps.append(m)
    res = run_bass_kernel_spmd(nc, in_maps, core_ids=list(range(NC)), **({"trace": True} if _trace else {}))
    R = res.results
    y_prompt = np.stack([R[b]["y_p"] for b in range(B)])
    y_sample = np.concatenate([R[c]["y_s"] for c in range(NC)])[:, None, :]
    ckv_p = np.stack([R[b]["o_ckv_p"] for b in range(B)], axis=1)
    kpe_p = np.stack([R[b]["o_kpe_p"] for b in range(B)], axis=1)
    ckv_s = np.concatenate([R[c]["o_ckv_s"] for c in range(NC)], axis=1)[:, :, None, :]
    kpe_s = np.concatenate([R[c]["o_kpe_s"] for c in range(NC)], axis=1)[:, :, None, :]
    conv_p = np.stack([R[b]["o_conv_p"] for b in range(B)], axis=1)
    conv_s = np.concatenate([R[c]["o_conv_s"].reshape(2, NS, 30, D) for c in range(NC)], axis=1)
    v_s = np.concatenate([R[c]["o_v_s"] for c in range(NC)], axis=1)[:, :, None, :]
    if _trace:
        kernel.last_exec_ns = res.exec_time_ns
    return (y_prompt, y_sample, ckv_p, kpe_p, ckv_s, kpe_s, conv_p, conv_s, v_s)
```

```python
import math
from contextlib import ExitStack
import numpy as np
import concourse.bass as bass
import concourse.mybir as mybir
from concourse.bass_utils import run_bass_kernel_spmd

F32 = mybir.dt.float32
BF16 = mybir.dt.bfloat16
I32 = mybir.dt.int32
AF = mybir.ActivationFunctionType
ALU = mybir.AluOpType

D = 1024
EPS = 1e-6
SCALE = 192.0 ** -0.5
FH = 2816
NJ = 22
USE_ACT_GELU = True


class Buf:
    def __init__(self, name):
        self.name = name
        self.w = {}
        self.r = {}
        self.dsem = None
        self.dcnt = 0


class Eng:
    def __init__(self, name):
        self.name = name
        self.sem = None
        self.cnt = 0
        self.seen = {}
        self.ops = []


class T:
    def __init__(self, t, units):
        self.t = t
        self.u = units

    def __getitem__(self, k):
        return self.t[k]


class Prog:
    def __init__(self, nc, stack):
        self.nc = nc
        self.stack = stack
        self.eng = {n: Eng(n) for n in ("pe", "act", "dve", "pool", "sp")}
        self.sems = {}
        for n, e in self.eng.items():
            e.sem = self.newsem("e_" + n)
        self.dry = False
        self.out_ev = {}
        self.nsem = 0

    def newsem(self, name):
        s = self.stack.enter_context(self.nc.semaphore(name))
        self.sems[id(s)] = s
        return s

    def sb(self, name, shape, dt, units=True):
        t = self.stack.enter_context(self.nc.sbuf_tensor(name, list(shape), dt))
        if units and len(shape) >= 3:
            u = [Buf("%s[%d]" % (name, i)) for i in range(shape[1])]
        else:
            u = [Buf(name)]
        return T(t, u)

    def _deps(self, e, reads, writes):
        waits = {}

        def need(d):
            for k, (sm, v) in d.items():
                if e.name == "pe" and sm is e.sem:
                    continue
                if e.seen.get(k, 0) < v:
                    if k not in waits or waits[k][1] < v:
                        waits[k] = (sm, v)
        for b in reads:
            need(b.w)
        for b in writes:
            need(b.w)
            need(b.r)
        for k, (sm, v) in waits.items():
            e.seen[k] = v
        return list(waits.values())

    def op(self, en, fn, reads=(), writes=()):
        if self.dry:
            return
        e = self.eng[en]
        waits = self._deps(e, reads, writes)
        e.cnt += 1
        e.ops.append((waits, fn, e.sem, 1))
        k = id(e.sem)
        for b in reads:
            b.r[k] = (e.sem, e.cnt)
        for b in writes:
            b.w = {k: (e.sem, e.cnt)}
            b.r = {}

    def dma(self, qn, fn, own, reads=(), writes=(), dreads=(), dappend=(), is_out=False):
        if self.dry:
            return
        e = self.eng[qn]
        wr = list(writes)
        if own not in wr:
            wr.append(own)
        rd = [b for b in reads if b is not own] + list(dreads)
        waits = self._deps(e, rd, wr)
        if own.dsem is None:
            own.dsem = self.newsem("d%d" % self.nsem)
            self.nsem += 1
        own.dcnt += 1
        k = id(own.dsem)
        ev = (own.dsem, 16 * own.dcnt)
        e.ops.append((waits, fn, own.dsem, 16))
        for b in reads:
            b.r[k] = ev
        for b in writes:
            b.w = {k: ev}
            b.r = {}
        if own not in reads and own not in writes:
            own.r[k] = ev
        for d in dappend:
            d.w[k] = ev
        if is_out:
            self.out_ev[k] = ev

    def finish(self):
        e = self.eng["sp"]
        waits = [ev for k, ev in self.out_ev.items()]
        e.ops.append((waits, None, None, 0))

    def emit(self, block):
        def mk(en):
            def f(h):
                for waits, fn, sem, inc in self.eng[en].ops:
                    for sm, v in waits:
                        h.wait_ge(sm, v)
                    if fn is not None:
                        ins = fn(h)
                        ins.then_inc(sem, inc)
            return f
        block.tensor(mk("pe"))
        block.scalar(mk("act"))
        block.vector(mk("dve"))
        block.gpsimd(mk("pool"))
        block.sync(mk("sp"))


class WStream:
    def __init__(self, P, nslots, depth, elems):
        self.P = P
        self.slots = [P.sb("wslot%d" % i, [128, elems], BF16) for i in range(nslots)]
        self.depth = depth
        self.seq = []
        self.pos = 0
        self.issued = 0

    def get(self, src, shape):
        if self.P.dry:
            self.seq.append((src, shape))
            sl = self.slots[0]
            p, a, b = shape
            return sl.t[:p, 0:a * b].rearrange("p (a b) -> p a b", a=a), sl.u[0]
        i = self.pos
        self.pos += 1
        while self.issued < min(len(self.seq), i + 1 + self.depth):
            self._issue(self.issued)
            self.issued += 1
        sl = self.slots[i % len(self.slots)]
        p, a, b = shape
        return sl.t[:p, 0:a * b].rearrange("p (a b) -> p a b", a=a), sl.u[0]

    def _issue(self, i):
        src, (p, a, b) = self.seq[i]
        sl = self.slots[i % len(self.slots)]
        dst = sl.t[:p, 0:a * b].rearrange("p (a b) -> p a b", a=a)
        self.P.dma("pool", lambda h, dst=dst, src=src: h.dma_start(out=dst, in_=src), sl.u[0], writes=[sl.u[0]])


def build_nc(SEQ, NS, NPG, NPOOL):
    TT = 512
    NTILE = SEQ // TT
    nc = bass.Bass("TRN2", target_bir_lowering=False)

    def din(name, shape, dt=F32):
        return nc.dram_tensor(name, list(shape), dt, kind="ExternalInput").ap()

    def dout(name, shape, dt=F32):
        return nc.dram_tensor(name, list(shape), dt, kind="ExternalOutput").ap()

    xp = din("xp", [SEQ, D])
    xs = din("xs", [NS, D])
    cpool = din("cpool", [2 * NPOOL * 128, 320])
    ptab = din("ptab", [1, NS * NPG], I32)
    stc = din("stc", [2, NS * 30, D])
    vecs = din("vecs", [78, D])
    gmg = din("gmg", [2, D])
    gmb = din("gmb", [2, D])
    kvn = din("kvn", [2, 256])
    wsp = din("wsp", [2, 4, 128, 128])
    bsp = din("bsp", [1, 2 * 512])
    w00 = din("w00", [1, 8])
    b00 = din("b00", [1, 8])
    w_in_ab = din("w_in_ab", [2, D, 4096])
    w_out_ab = din("w_out_ab", [2, 2048, D])
    w_in_c = din("w_in_c", [2, D, 832])
    w_uq = din("w_uq", [2, 512, 1536])
    w_uk = din("w_uk", [2, 256, 1024])
    w_uv = din("w_uv", [2, 256, 1024])
    w_out_c = din("w_out_c", [2, D, D])
    w_gate = din("w_gate", [4, D, FH])
    w_up = din("w_up", [4, D, FH])
    w_down = din("w_down", [4, FH, D])
    c_ident = din("c_ident", [128, 128])
    c_tri = din("c_tri", [128, 128])
    c_cs_tm = din("c_cs_tm", [SEQ, 64])
    c_cs_fm = din("c_cs_fm", [2, 64, SEQ])
    c_cs_s = din("c_cs_s", [1, 64])
    c_cs_fs = din("c_cs_fs", [2, 64, NS])

    y_p = dout("y_p", [SEQ, D])
    y_s = dout("y_s", [NS, D])
    o_ckv_p = dout("o_ckv_p", [2, SEQ, 256])
    o_kpe_p = dout("o_kpe_p", [2, SEQ, 64])
    o_ckv_s = dout("o_ckv_s", [2, NS, 256])
    o_kpe_s = dout("o_kpe_s", [2, NS, 64])
    o_conv_p = dout("o_conv_p", [2, 30, D])
    o_conv_s = dout("o_conv_s", [2, NS * 30, D])
    o_v_s = dout("o_v_s", [2, NS, D])

    Kn_d = nc.dram_tensor("Kn_d", [2, 8, 128, SEQ], BF16).ap()
    Kp_d = nc.dram_tensor("Kp_d", [2, 64, SEQ], BF16).ap()
    V_d = nc.dram_tensor("V_d", [2, SEQ, D], BF16).ap()
    dKn = [Buf("dKn0"), Buf("dKn1")]
    dKp = [Buf("dKp0"), Buf("dKp1")]
    dV = [Buf("dV0"), Buf("dV1")]

    stack = ExitStack()
    with stack:
        P = Prog(nc, stack)
        X = P.sb("X", [128, 8, TT], F32)
        H = P.sb("H", [128, 8, TT], BF16)
        SQ = P.sb("SQ", [128, 2, TT], BF16)
        A = P.sb("A", [128, NJ, TT], BF16)
        ACC = P.sb("ACC", [128, 8, TT], F32)
        XE = P.sb("XE", [128, 2, 64], F32)
        XEB = P.sb("XEB", [128, 2, 30 + TT], BF16)
        DG = P.sb("DG", [128, 2, 16 * 128], BF16)
        CARRY = P.sb("CARRY", [128, 2, 8 * 30], F32)
        BO = P.sb("BO", [128, 8, TT], BF16)
        G = P.sb("G", [128, 2, 1024], F32)
        T1 = P.sb("T1", [128, 2, TT], F32)
        SG = P.sb("SG", [128, 2, TT], F32)
        TOK = P.sb("TOK", [128, 4, 320], F32)
        STAT = P.sb("STAT", [128, 3, TT], F32)
        SM = P.sb("SM", [128, 8, 8], F32)
        GLN = P.sb("GLN", [128, 2, 1024], F32)
        KVN = P.sb("KVN", [128, 2, 256], F32, units=False)
        PTb = P.sb("PTb", [128, 2, TT], BF16)
        KSEG = 512
        KN = P.sb("KN", [128, 2, KSEG], BF16)
        VV = P.sb("VV", [128, 2, KSEG], BF16)
        KP = P.sb("KP", [128, 2, KSEG], BF16)
        CSF = P.sb("CSF", [64, 2, TT], F32)
        CST = P.sb("CST", [128, 4, 64], F32, units=False)
        VECT = P.sb("VECT", [128, 8, 78], F32, units=False)
        WST = P.sb("WST", [128, 8, 128], BF16, units=False)
        W00I = P.sb("W00I", [16, 8, 16], BF16, units=False)
        BC8 = P.sb("BC8", [128, 2, 8], F32, units=False)
        BSP = P.sb("BSP", [1, 1024], F32, units=False)
        ONE1 = P.sb("ONE1", [1, 128], F32, units=False)
        IDENT = P.sb("IDENT", [128, 128], F32, units=False)
        IDENTB = P.sb("IDENTB", [128, 128], BF16, units=False)
        TRI = P.sb("TRI", [128, 128], F32, units=False)
        TRIB = P.sb("TRIB", [128, 128], BF16, units=False)
        ONESB = P.sb("ONESB", [128, 128], BF16, units=False)
        WR = P.sb("WR", [128, 4, 256], BF16, units=False)
        STG = P.sb("STG", [128, 2, 1024], F32)
        VST = P.sb("VST", [128, 2, 1024], BF16)
        IOTAF = P.sb("IOTAF", [128, 1], F32, units=False)
        DMY = Buf("dmy")
        RSC = P.sb("RSC", [128, 64], F32, units=False)
        WUKT = T(VST.t[:, :, :].rearrange("p a (b c) -> p (a b) c", c=256), VST.u)
        GP = 2
        NCK = 4
        CK = P.sb("CK", [128, NCK, GP * 324], BF16)
        KT = P.sb("KT", [128, 3, 3 * GP * 128], BF16)
        IDX = P.sb("IDX", [128, 2, NPG], I32)
        IOTA = P.sb("IOTA", [128, 1], I32, units=False)
        PSS = P.sb("PSS", [128, 2, GP * 8], BF16)
        ONEC = P.sb("ONEC", [128, 2], BF16, units=False)
        PN = P.sb("PN", [1, 2, 8], F32)
        NEWROW = P.sb("NEWROW", [1, 2, 260], F32)
        QL = P.sb("QL", [128, 2 * NS * 8], BF16, units=False)
        QPS = P.sb("QPS", [64, NS * 8], BF16, units=False)
        OL = P.sb("OL", [8, 2, 256], F32)
        OLT = P.sb("OLT", [128, 2 * 8 * NS], BF16, units=False)
        ws = WStream(P, 4, 2, 4096)
        PS = []
        for i in range(8):
            t = stack.enter_context(nc.psum_tensor("ps%d" % i, [128, 512], F32))
            PS.append(T(t, [Buf("ps%d" % i)]))
        pst = {"i": 0, "n": 8}

        def psum():
            i = pst["i"]
            pst["i"] = (i + 1) % 8
            if pst["n"] == 4:
                return PS[4 + i % 4]
            return PS[i]

        def mm(ps, rows, cols, pairs, reads, start=True, stop=True):
            def fn(h, ps=ps, rows=rows, cols=cols, pairs=pairs):
                ins = None
                n = len(pairs)
                for i, (l, r) in enumerate(pairs):
                    ins = h.matmul(ps.t[rows, cols], l, r, start=(start and i == 0), stop=(stop and i == n - 1))
                return ins
            P.op("pe", fn, reads=reads, writes=ps.u)

        def act(out, in_, func, reads, writes, scale=1.0, bias=0.0, accum=None):
            def fn(h):
                kw = {}
                if accum is not None:
                    kw["accum_out"] = accum
                return h.activation(out=out, in_=in_, func=func, bias=bias, scale=scale, **kw)
            P.op("act", fn, reads=reads, writes=writes)

        def tt(out, a, b, op, reads, writes, eng="dve"):
            P.op(eng, lambda h: h.tensor_tensor(out=out, in0=a, in1=b, op=op), reads=reads, writes=writes)

        def ts(out, a, s1, s2, op0, op1, reads, writes, eng="dve"):
            if op1 is None:
                P.op(eng, lambda h: h.tensor_scalar(out=out, in0=a, scalar1=s1, scalar2=None, op0=op0), reads=reads, writes=writes)
            else:
                P.op(eng, lambda h: h.tensor_scalar(out=out, in0=a, scalar1=s1, scalar2=s2, op0=op0, op1=op1), reads=reads, writes=writes)

        def stt(out, a, s, b, op0, op1, reads, writes, eng="dve"):
            P.op(eng, lambda h: h.scalar_tensor_tensor(out=out, in0=a, scalar=s, in1=b, op0=op0, op1=op1), reads=reads, writes=writes)

        def copy(out, in_, reads, writes, eng="act"):
            if eng == "act":
                P.op("act", lambda h: h.copy(out=out, in_=in_), reads=reads, writes=writes)
            else:
                P.op(eng, lambda h: h.tensor_copy(out=out, in_=in_), reads=reads, writes=writes)

        def recip(out, in_, reads, writes):
            P.op("dve", lambda h: h.reciprocal(out=out, in_=in_), reads=reads, writes=writes)

        def dma(q, out, in_, own, **kw):
            P.dma(q, lambda h: h.dma_start(out=out, in_=in_), own, **kw)

        def gelu_to(out, ps, rows, n, wunits):
            src = ps.t[rows, 0:n]
            if USE_ACT_GELU:
                act(out, src, AF.Gelu_apprx_tanh, ps.u, wunits)
                return
            k = gelu_to.k = (getattr(gelu_to, "k", 0) + 1) % 2
            t1 = T1.t[rows, k, 0:n]
            act(t1, src, AF.Square, ps.u, [T1.u[k]])
            ts(t1, t1, 0.044715, 1.0, ALU.mult, ALU.add, [T1.u[k]], [T1.u[k]])
            tt(t1, t1, src, ALU.mult, [T1.u[k]] + ps.u, [T1.u[k]])
            act(t1, t1, AF.Sigmoid, [T1.u[k]], [T1.u[k]], scale=1.5957691216057308)
            tt(out, t1, src, ALU.mult, [T1.u[k]] + ps.u, wunits)

        def rms_fm(Xt, nch, gain, Ht, hoff, n, dim):
            ps = psum()
            for c in range(nch):
                k = c % 2
                act(SQ.t[:, k, 0:n], Xt.t[:, c, 0:n], AF.Square, [Xt.u[c]], [SQ.u[k]])
                mm(ps, slice(0, 128), slice(0, n), [(ONESB.t[:, :], SQ.t[:, k, 0:n])], [SQ.u[k]] + ONESB.u,
                   start=(c == 0), stop=(c == nch - 1))
            rs = STAT.t[:, 0, 0:n]
            act(rs, ps.t[:, 0:n], AF.Sqrt, ps.u, [STAT.u[0]], scale=1.0 / dim, bias=EPS)
            recip(rs, rs, [STAT.u[0]], [STAT.u[0]])
            for c in range(nch):
                stt(Ht.t[:, hoff + c, 0:n], Xt.t[:, c, 0:n], gain(c), rs, ALU.mult, ALU.mult,
                    [Xt.u[c], STAT.u[0]] + VECT.u, [Ht.u[hoff + c]])

        def wcols(w2d, c0, c1, kch):
            return w2d[:, c0:c1].rearrange("(k p) c -> p k c", p=128), (128, kch, c1 - c0)

        def fm_to_tm(src_fn, src_reads, nch, rows, dst, q="sp", is_out=True):
            k = fm_to_tm.k = (getattr(fm_to_tm, "k", 0) + 1) % 2
            for half in range((nch + 3) // 4):
                ps = psum()
                cs = list(range(half * 4, min(nch, half * 4 + 4)))
                for i, c in enumerate(cs):
                    mm(ps, slice(0, rows), slice(i * 128, (i + 1) * 128), [(src_fn(c), IDENT.t[:, :])],
                       src_reads(c) + IDENT.u)
                copy(STG.t[0:rows, k, half * 512:half * 512 + 128 * len(cs)], ps.t[0:rows, 0:128 * len(cs)],
                     ps.u, [STG.u[k]])
            dma(q, dst, STG.t[0:rows, k, 0:nch * 128], STG.u[k], reads=[STG.u[k]], is_out=is_out)

        def ffn(L, n):
            rms_fm(X, 8, lambda c: VECT.t[:, c, 4 + L:5 + L], H, 0, n, D)
            for jg in range(6):
                c0, c1 = jg * 512, min(FH, jg * 512 + 512)
                Wg, bg = ws.get(*wcols(w_gate[L], c0, c1, 8))
                Wu, bu = ws.get(*wcols(w_up[L], c0, c1, 8))
                for m in range((c1 - c0) // 128):
                    j = jg * 4 + m
                    pg = psum()
                    mm(pg, slice(0, 128), slice(0, n), [(Wg[:, k, m * 128:(m + 1) * 128], H.t[:, k, 0:n]) for k in range(8)], H.u + [bg])
                    pu = psum()
                    mm(pu, slice(0, 128), slice(0, n), [(Wu[:, k, m * 128:(m + 1) * 128], H.t[:, k, 0:n]) for k in range(8)], H.u + [bu])
                    k2 = j % 2
                    act(SG.t[:, k2, 0:n], pg.t[:, 0:n], AF.Silu, pg.u, [SG.u[k2]])
                    tt(A.t[:, j, 0:n], SG.t[:, k2, 0:n], pu.t[:, 0:n], ALU.mult, [SG.u[k2]] + pu.u, [A.u[j]])
            for m in range(8):
                Wd, bd = ws.get(w_down[L][:, m * 128:(m + 1) * 128].rearrange("(j p) c -> p j c", p=128), (128, NJ, 128))
                ps = psum()
                mm(ps, slice(0, 128), slice(0, n), [(Wd[:, j, :], A.t[:, j, 0:n]) for j in range(NJ)], A.u + [bd])
                tt(X.t[:, m, 0:n], X.t[:, m, 0:n], ps.t[:, 0:n], ALU.add, [X.u[m]] + ps.u, [X.u[m]])

        def even_layer(e, L, n, sample, tile_idx):
            nt = max(1, n // 128)
            pt = min(n, 128)
            U = lambda c: A.t[:, c, 0:n]
            AO = lambda c: A.t[:, 8 + c, 0:n]
            rms_fm(X, 8, lambda c: VECT.t[:, c, L:L + 1], H, 0, n, D)
            dma("sp", GLN.t[:, 0, :], gmg[e:e + 1, :].partition_broadcast(128), GLN.u[0], writes=[GLN.u[0]])
            dma("sp", GLN.t[:, 1, :], gmb[e:e + 1, :].partition_broadcast(128), GLN.u[1], writes=[GLN.u[1]])
            for g4 in range(2):
                W, bw = ws.get(*wcols(w_in_ab[e], g4 * 512, g4 * 512 + 512, 8))
                for m in range(4):
                    c = g4 * 4 + m
                    ps = psum()
                    mm(ps, slice(0, 128), slice(0, n), [(W[:, k, m * 128:(m + 1) * 128], H.t[:, k, 0:n]) for k in range(8)], H.u + [bw])
                    gelu_to(U(c), ps, slice(0, 128), n, [A.u[c]])
            W0, b0 = ws.get(*wcols(w_in_ab[e], 1024, 1536, 8))
            W1, b1 = ws.get(*wcols(w_in_ab[e], 1536, 2048, 8))
            if True:
                for q in range(nt):
                    gk = q % 2
                    for half, (W, bw) in enumerate(((W0, b0), (W1, b1))):
                        ps = psum()
                        mm(ps, slice(0, pt), slice(0, 512), [(H.t[:, k, q * 128:q * 128 + pt], W[:, k, :]) for k in range(8)], H.u + [bw])
                        gelu_to(G.t[0:pt, gk, half * 512:(half + 1) * 512], ps, slice(0, pt), 512, [G.u[gk]])
                    g = G.t[0:pt, gk, :]
                    st = SM.t[0:pt, 0, :]
                    P.op("dve", lambda h, g=g: h.bn_stats(out=SM.t[0:pt, 0, 0:6], in_=g[:, 0:512]), reads=[G.u[gk]], writes=[SM.u[0]])
                    P.op("dve", lambda h, g=g: h.bn_stats(out=SM.t[0:pt, 1, 0:6], in_=g[:, 512:1024]), reads=[G.u[gk]], writes=[SM.u[1]])
                    P.op("dve", lambda h: h.bn_aggr(out=SM.t[0:pt, 2, 0:2], in_=SM.t[0:pt, 0:2, 0:6]), reads=[SM.u[0], SM.u[1]], writes=[SM.u[2]])
                    act(SM.t[0:pt, 2, 2:3], SM.t[0:pt, 2, 1:2], AF.Sqrt, [SM.u[2]], [SM.u[2]], bias=EPS)
                    recip(SM.t[0:pt, 2, 3:4], SM.t[0:pt, 2, 2:3], [SM.u[2]], [SM.u[2]])
                    ts(g, g, SM.t[0:pt, 2, 0:1], SM.t[0:pt, 2, 3:4], ALU.subtract, ALU.mult, [G.u[gk], SM.u[2]], [G.u[gk]])
                    tt(g, g, GLN.t[0:pt, 0, :], ALU.mult, [G.u[gk], GLN.u[0]], [G.u[gk]])
                    tt(g, g, GLN.t[0:pt, 1, :], ALU.add, [G.u[gk], GLN.u[1]], [G.u[gk]])
                    vq = BO.t[0:pt, 2 * q:2 * q + 2, :].rearrange("p a b -> p (a b)")
                    copy(vq, g, [G.u[gk]], [BO.u[2 * q], BO.u[2 * q + 1]], eng="act")
                    if sample:
                        dma("sp", o_v_s[e], g, G.u[gk], reads=[G.u[gk]], is_out=True)
                for c in range(8):
                    gi = c // 2
                    ps = psum()
                    for q in range(nt):
                        vq = BO.t[0:pt, 2 * q:2 * q + 2, :].rearrange("p a b -> p (a b)")
                        lhs = vq[:, c * 128:(c + 1) * 128]
                        if sample:
                            mm(ps, slice(0, 128), slice(0, n), [(lhs, W00I.t[0:pt, e * 4 + gi, 0:pt])],
                               [BO.u[2 * q], BO.u[2 * q + 1]] + W00I.u)
                        else:
                            def fn(h, ps=ps, q=q, lhs=lhs, gi=gi):
                                h.matmul(ps.t[:, q * 128:(q + 1) * 128], lhs, WST.t[:, e * 4 + gi, :], start=True, stop=False)
                                return h.matmul(ps.t[:, q * 128:(q + 1) * 128], ONE1.t[0:1, :],
                                                BSP.t[0:1, e * 512 + gi * 128:e * 512 + gi * 128 + 128], start=False, stop=True)
                            P.op("pe", fn, reads=[BO.u[2 * q], BO.u[2 * q + 1]] + WST.u + BSP.u + ONE1.u, writes=ps.u)
                    if sample:
                        stt(AO(c), ps.t[:, 0:n], BC8.t[:, 1, e * 4 + gi:e * 4 + gi + 1], U(c), ALU.add, ALU.mult,
                            ps.u + [A.u[c]] + BC8.u, [A.u[8 + c]])
                    else:
                        tt(AO(c), ps.t[:, 0:n], U(c), ALU.mult, ps.u + [A.u[c]], [A.u[8 + c]])
            if sample:
                rows = NS * 30
                r0 = 0
                while r0 < rows:
                    rr = min(120, rows - r0)
                    gk = (r0 // 120) % 2
                    dma("sp", G.t[0:rr, gk, :], stc[e, r0:r0 + rr, :], G.u[gk], writes=[G.u[gk]])
                    for half in range(2):
                        ps = psum()
                        for i in range(4):
                            c = half * 4 + i
                            mm(ps, slice(0, 128), slice(i * 128, i * 128 + rr), [(G.t[0:rr, gk, c * 128:(c + 1) * 128], IDENT.t[0:rr, 0:rr])],
                               [G.u[gk]] + IDENT.u)
                        for i in range(4):
                            c = half * 4 + i
                            copy(ACC.t[:, c, 32 + r0:32 + r0 + rr], ps.t[:, i * 128:i * 128 + rr], ps.u, [ACC.u[c]])
                    r0 += rr
                dmy = DMY
                for b in range(NS):
                    dma("sp", o_conv_s[e, b * 30:b * 30 + 29, :], stc[e, b * 30 + 1:b * 30 + 30, :], dmy, is_out=True)
            for g4 in range(2):
                Wb, bb = ws.get(*wcols(w_in_ab[e], 2048 + g4 * 512, 2048 + g4 * 512 + 512, 8))
                Wg, bg = ws.get(*wcols(w_in_ab[e], 3072 + g4 * 512, 3072 + g4 * 512 + 512, 8))
                for m in range(4):
                    c = g4 * 4 + m
                    k2 = c % 2
                    pb = psum()
                    mm(pb, slice(0, 128), slice(0, n), [(Wb[:, k, m * 128:(m + 1) * 128], H.t[:, k, 0:n]) for k in range(8)], H.u + [bb])
                    pg = psum()
                    mm(pg, slice(0, 128), slice(0, n), [(Wg[:, k, m * 128:(m + 1) * 128], H.t[:, k, 0:n]) for k in range(8)], H.u + [bg])
                    act(SG.t[:, k2, 0:n], pg.t[:, 0:n], AF.Sigmoid, pg.u, [SG.u[k2]])
                    cw = lambda k, c=c: VECT.t[:, c, 15 + e * 31 + k:16 + e * 31 + k]
                    cb = VECT.t[:, c, 9 + e:10 + e]
                    acc = ACC.t[:, c, 0:n]
                    if sample:
                        glu = XE.t[:, k2, 0:n]
                        tt(glu, SG.t[:, k2, 0:n], pb.t[:, 0:n], ALU.mult, [SG.u[k2]] + pb.u, [XE.u[k2]])
                        st3 = ACC.t[:, c, 32:32 + NS * 30].rearrange("p (b k) -> p b k", k=30)
                        wv = VECT.t[:, c, 15 + e * 31:15 + e * 31 + 30]
                        tmp = T1.t[:, k2, 0:NS * 30].rearrange("p (b k) -> p b k", k=30)
                        P.op("dve", lambda h, tmp=tmp, st3=st3, wv=wv: h.tensor_tensor(
                            out=tmp, in0=st3, in1=wv.unsqueeze(1).broadcast_to([128, NS, 30]), op=ALU.mult),
                            reads=[ACC.u[c]] + VECT.u, writes=[T1.u[k2]])
                        P.op("dve", lambda h, tmp=tmp, k2=k2: h.tensor_reduce(
                            out=STAT.t[:, 1 + k2, 0:NS], in_=tmp, axis=mybir.AxisListType.X, op=ALU.add),
                            reads=[T1.u[k2]], writes=[STAT.u[1 + k2]])
                        stt(acc, glu, cw(30), STAT.t[:, 1 + k2, 0:NS], ALU.mult, ALU.add, [XE.u[k2], STAT.u[1 + k2]] + VECT.u, [ACC.u[c]])
                        ts(acc, acc, cb, None, ALU.add, None, [ACC.u[c]] + VECT.u, [ACC.u[c]])
                        copy(CARRY.t[:, e, c * 30:c * 30 + n], glu, [XE.u[k2]], [CARRY.u[e]], eng="act")
                    else:
                        xe = XEB.t[:, k2, :]
                        copy(xe[:, 0:30], CARRY.t[:, e, c * 30:(c + 1) * 30], [CARRY.u[e]], [XEB.u[k2]], eng="act")
                        tt(xe[:, 30:30 + n], SG.t[:, k2, 0:n], pb.t[:, 0:n], ALU.mult, [SG.u[k2]] + pb.u, [XEB.u[k2]])
                        tt(CARRY.t[:, e, c * 30:(c + 1) * 30], SG.t[:, k2, n - 30:n], pb.t[:, n - 30:n], ALU.mult, [SG.u[k2]] + pb.u, [CARRY.u[e]])
                        pc = psum()
                        for u in range(2):
                            taps = list(range(16)) if u == 0 else list(range(16, 31))
                            nt_ = len(taps)
                            wv_ = VECT.t[:, c, 15 + e * 31 + taps[0]:15 + e * 31 + taps[0] + nt_]
                            dgv = DG.t[:, u, 0:nt_ * 128].rearrange("p (k j) -> p k j", j=128)
                            P.op("dve", lambda h_, dgv=dgv, wv_=wv_, nt_=nt_: h_.tensor_tensor(
                                out=dgv, in0=IDENTB.t[:, :].unsqueeze(1).broadcast_to([128, nt_, 128]),
                                in1=wv_.unsqueeze(2).broadcast_to([128, nt_, 128]), op=ALU.mult),
                                reads=IDENTB.u + VECT.u, writes=[DG.u[u]])
                            mm(pc, slice(0, 128), slice(0, n), [(DG.t[:, u, kk * 128:(kk + 1) * 128], xe[:, k:k + n]) for kk, k in enumerate(taps)],
                               [DG.u[u], XEB.u[k2]], start=(u == 0), stop=(u == 1))
                        act(acc, pc.t[:, 0:n], AF.Identity, pc.u + VECT.u, [ACC.u[c]], bias=cb)
            if True:
                p1 = psum()
                p2 = psum()
                for c in range(8):
                    k = c % 2
                    copy(SQ.t[:, k, 0:n], ACC.t[:, c, 0:n], [ACC.u[c]], [SQ.u[k]], eng="act")
                    mm(p1, slice(0, 128), slice(0, n), [(ONESB.t[:, :], SQ.t[:, k, 0:n])], [SQ.u[k]] + ONESB.u, start=(c == 0), stop=(c == 7))
                    act(SQ.t[:, k, 0:n], ACC.t[:, c, 0:n], AF.Square, [ACC.u[c]], [SQ.u[k]])
                    mm(p2, slice(0, 128), slice(0, n), [(ONESB.t[:, :], SQ.t[:, k, 0:n])], [SQ.u[k]] + ONESB.u, start=(c == 0), stop=(c == 7))
                mu = STAT.t[:, 1, 0:n]
                rs = STAT.t[:, 2, 0:n]
                ts(mu, p1.t[:, 0:n], 1.0 / D, None, ALU.mult, None, p1.u, [STAT.u[1]])
                tt(rs, mu, mu, ALU.mult, [STAT.u[1]], [STAT.u[2]])
                stt(rs, p2.t[:, 0:n], 1.0 / D, rs, ALU.mult, ALU.subtract, p2.u + [STAT.u[2]], [STAT.u[2]])
                act(rs, rs, AF.Sqrt, [STAT.u[2]], [STAT.u[2]], bias=EPS)
                recip(rs, rs, [STAT.u[2]], [STAT.u[2]])
                for c in range(8):
                    acc = ACC.t[:, c, 0:n]
                    tt(acc, acc, mu, ALU.subtract, [ACC.u[c], STAT.u[1]], [ACC.u[c]])
                    tt(acc, acc, rs, ALU.mult, [ACC.u[c], STAT.u[2]], [ACC.u[c]])
                    act(BO.t[:, c, 0:n], acc, AF.Silu, [ACC.u[c]] + VECT.u, [BO.u[c]],
                        scale=VECT.t[:, c, 11 + e:12 + e], bias=VECT.t[:, c, 13 + e:14 + e])
            for g4 in range(4):
                W, bw = ws.get(*wcols(w_out_ab[e], g4 * 256, g4 * 256 + 256, 16))
                for m in range(2):
                    mc = g4 * 2 + m
                    ps = psum()
                    pairs = [(W[:, k, m * 128:(m + 1) * 128], A.t[:, 8 + k, 0:n]) for k in range(8)]
                    pairs += [(W[:, 8 + k, m * 128:(m + 1) * 128], BO.t[:, k, 0:n]) for k in range(8)]
                    mm(ps, slice(0, 128), slice(0, n), pairs, A.u[8:16] + BO.u + [bw])
                    tt(X.t[:, mc, 0:n], X.t[:, mc, 0:n], ps.t[:, 0:n], ALU.add, [X.u[mc]] + ps.u, [X.u[mc]])
            if True:
                if sample:
                    fm_to_tm(lambda c: CARRY.t[:, e, c * 30:c * 30 + n], lambda c: [CARRY.u[e]], 8, n,
                             o_conv_s[e].rearrange("(b k) d -> b k d", k=30)[:, 29, :])
                elif tile_idx == NTILE - 1:
                    fm_to_tm(lambda c: CARRY.t[:, e, c * 30:(c + 1) * 30], lambda c: [CARRY.u[e]], 8, 30, o_conv_p[e])

        def odd_layer(o, L, n, sample, tile_idx):
            nt = max(1, n // 128)
            pt = min(n, 128)
            t0 = tile_idx * TT
            QN = lambda h: A.t[:, h, 0:n]
            OH = lambda h: A.t[:, 8 + h, 0:n]
            CQN = lambda c: A.t[:, 16 + c, 0:n]
            CKT = lambda c: A.t[:, 20 + c, 0:n]
            rms_fm(X, 8, lambda c: VECT.t[:, c, L:L + 1], H, 0, n, D)
            Wc, bc = ws.get(*wcols(w_in_c[o], 0, 512, 8))
            if True:
                if sample:
                    dma("sp", CST.t[0:pt, 0, :], c_cs_s.partition_broadcast(pt), CST.u[0], writes=CST.u)
                    dma("sp", CSF.t[:, 0, 0:n], c_cs_fs[0], CSF.u[0], writes=[CSF.u[0]])
                    dma("sp", CSF.t[:, 1, 0:n], c_cs_fs[1], CSF.u[1], writes=[CSF.u[1]])
                else:
                    dma("sp", CST.t[:, :, :], c_cs_tm[t0:t0 + n, :].rearrange("(q p) f -> p q f", p=128), CST.u[0], writes=CST.u)
                    dma("sp", CSF.t[:, 0, 0:n], c_cs_fm[0, :, t0:t0 + n], CSF.u[0], writes=[CSF.u[0]])
                    dma("sp", CSF.t[:, 1, 0:n], c_cs_fm[1, :, t0:t0 + n], CSF.u[1], writes=[CSF.u[1]])
                for m in range(4):
                    ps = psum()
                    mm(ps, slice(0, 128), slice(0, n), [(Wc[:, k, m * 128:(m + 1) * 128], H.t[:, k, 0:n]) for k in range(8)], H.u + [bc])
                    copy(ACC.t[:, m, 0:n], ps.t[:, 0:n], ps.u, [ACC.u[m]])
                rms_fm(ACC, 4, lambda c: VECT.t[:, 4 * o + c, 77:78], A, 16, n, 512)
                Wk, bk = ws.get(*wcols(w_in_c[o], 512, 832, 8))
                for q in range(nt):
                    ps = psum()
                    mm(ps, slice(0, pt), slice(0, 320), [(H.t[:, k, q * 128:q * 128 + pt], Wk[:, k, :]) for k in range(8)], H.u + [bk])
                    tok = TOK.t[0:pt, q, :]
                    sm = SM.t[0:pt, 3, :]
                    act(T1.t[0:pt, 0, 0:256], ps.t[0:pt, 0:256], AF.Square, ps.u, [T1.u[0], SM.u[3]], accum=SM.t[0:pt, 3, 0:1])
                    act(SM.t[0:pt, 3, 1:2], SM.t[0:pt, 3, 0:1], AF.Sqrt, [SM.u[3]], [SM.u[3]], scale=1.0 / 256, bias=EPS)
                    recip(SM.t[0:pt, 3, 2:3], SM.t[0:pt, 3, 1:2], [SM.u[3]], [SM.u[3]])
                    stt(tok[:, 0:256], ps.t[0:pt, 0:256], SM.t[0:pt, 3, 2:3], KVN.t[0:pt, o, :], ALU.mult, ALU.mult,
                        ps.u + [SM.u[3]] + KVN.u, [TOK.u[q]])
                    x1 = ps.t[0:pt, 256:288]
                    x2 = ps.t[0:pt, 288:320]
                    cq_ = 0 if sample else q
                    cs_ = CST.t[0:pt, cq_, 0:32]
                    sn_ = CST.t[0:pt, cq_, 32:64]
                    sc = RSC.t[0:pt, :]
                    smu = RSC.u
                    tt(sc[:, 0:32], x1, cs_, ALU.mult, ps.u + CST.u + [SM.u[3]], smu)
                    tt(sc[:, 32:64], x2, sn_, ALU.mult, ps.u + CST.u, smu)
                    tt(tok[:, 256:288], sc[:, 0:32], sc[:, 32:64], ALU.subtract, smu, [TOK.u[q]])
                    tt(sc[:, 0:32], x1, sn_, ALU.mult, ps.u + CST.u + [TOK.u[q]], smu)
                    tt(sc[:, 32:64], x2, cs_, ALU.mult, ps.u + CST.u, smu)
                    tt(tok[:, 288:320], sc[:, 0:32], sc[:, 32:64], ALU.add, smu, [TOK.u[q]])
                    if sample:
                        dma("sp", o_ckv_s[o], tok[:, 0:256], TOK.u[q], reads=[TOK.u[q]], is_out=True)
                        dma("sp", o_kpe_s[o], tok[:, 256:320], TOK.u[q], reads=[TOK.u[q]], is_out=True)
                    else:
                        r0 = t0 + q * 128
                        dma("sp", o_ckv_p[o, r0:r0 + pt, :], tok[:, 0:256], TOK.u[q], reads=[TOK.u[q]], is_out=True)
                        dma("sp", o_kpe_p[o, r0:r0 + pt, :], tok[:, 256:320], TOK.u[q], reads=[TOK.u[q]], is_out=True)
                    for cc in range(2):
                        p2 = psum()
                        mm(p2, slice(0, 128), slice(0, pt), [(tok[:, cc * 128:(cc + 1) * 128], IDENT.t[0:pt, 0:pt])], [TOK.u[q]] + IDENT.u)
                        copy(A.t[:, 20 + cc, q * 128:q * 128 + pt], p2.t[:, 0:pt], p2.u, [A.u[20 + cc]])
                    p3 = psum()
                    mm(p3, slice(0, 64), slice(0, pt), [(tok[:, 256:320], IDENT.t[0:pt, 0:pt])], [TOK.u[q]] + IDENT.u)
                    copy(SQ.t[0:64, 0, q * 128:q * 128 + pt], p3.t[0:64, 0:pt], p3.u, [SQ.u[0]])
                if not sample:
                    Wuk, buk = ws.get(w_uk[o].rearrange("(k p) c -> p k c", p=128), (128, 2, 1024))
                    for h in range(8):
                        ps = psum()
                        mm(ps, slice(0, 128), slice(0, n), [(Wuk[:, cc, h * 128:(h + 1) * 128], CKT(cc)) for cc in range(2)], [A.u[20], A.u[21], buk])
                        copy(BO.t[:, h, 0:n], ps.t[:, 0:n], ps.u, [BO.u[h]])
                        dma("sp", Kn_d[o, h, :, t0:t0 + n], BO.t[:, h, 0:n], BO.u[h], reads=[BO.u[h]], dappend=[dKn[o]])
                    dma("sp", Kp_d[o, :, t0:t0 + n], SQ.t[0:64, 0, 0:n], SQ.u[0], reads=[SQ.u[0]], dappend=[dKp[o]])
                    Wuv, buv = ws.get(w_uv[o].rearrange("(k p) c -> p k c", p=128), (128, 2, 1024))
                    for q in range(nt):
                        for half in range(2):
                            ps = psum()
                            mm(ps, slice(0, pt), slice(0, 512), [(A.t[:, 20 + cc, q * 128:q * 128 + pt], Wuv[:, cc, half * 512:(half + 1) * 512]) for cc in range(2)],
                               [A.u[20], A.u[21], buv])
                            copy(VST.t[0:pt, q % 2, half * 512:(half + 1) * 512], ps.t[0:pt, 0:512], ps.u, [VST.u[q % 2]])
                        dma("sp", V_d[o, t0 + q * 128:t0 + q * 128 + pt, :], VST.t[0:pt, q % 2, :], VST.u[q % 2],
                            reads=[VST.u[q % 2]], dappend=[dV[o]])
            for hg in range(2):
                Wq, bq = ws.get(*wcols(w_uq[o], hg * 768, hg * 768 + 768, 4))
                for hh in range(4):
                    src1 = Wq[:, :, hh * 192 + 160:hh * 192 + 192]
                    src2 = Wq[:, :, hh * 192 + 128:hh * 192 + 160]
                    P.op("dve", lambda h_, hh=hh, src1=src1: h_.tensor_scalar(out=WR.t[:, :, hh * 64:hh * 64 + 32], in0=src1, scalar1=-1.0, scalar2=None, op0=ALU.mult),
                         reads=[bq], writes=WR.u)
                    P.op("dve", lambda h_, hh=hh, src2=src2: h_.tensor_copy(out=WR.t[:, :, hh * 64 + 32:hh * 64 + 64], in_=src2), reads=[bq], writes=WR.u)
                for hh in range(4):
                    h = hg * 4 + hh
                    cqr = [A.u[16 + k] for k in range(4)]
                    ps = psum()
                    mm(ps, slice(0, 128), slice(0, n), [(Wq[:, k, hh * 192:hh * 192 + 128], CQN(k)) for k in range(4)], cqr + [bq])
                    copy(QN(h), ps.t[:, 0:n], ps.u, [A.u[h]])
                    pa = psum()
                    mm(pa, slice(0, 64), slice(0, n), [(Wq[:, k, hh * 192 + 128:hh * 192 + 192], CQN(k)) for k in range(4)], cqr + [bq])
                    pb = psum()
                    mm(pb, slice(0, 64), slice(0, n), [(WR.t[:, k, hh * 64:(hh + 1) * 64], CQN(k)) for k in range(4)], cqr + WR.u)
                    ta = T1.t[0:64, 0, 0:n]
                    tb = T1.t[0:64, 1, 0:n]
                    tt(ta, pa.t[0:64, 0:n], CSF.t[:, 0, 0:n], ALU.mult, pa.u + [CSF.u[0]], [T1.u[0]])
                    tt(tb, pb.t[0:64, 0:n], CSF.t[:, 1, 0:n], ALU.mult, pb.u + [CSF.u[1]], [T1.u[1]])
                    tt(H.t[0:64, h, 0:n], ta, tb, ALU.add, [T1.u[0], T1.u[1]], [H.u[h]])
            if not sample:
                nblk = 4 * tile_idx + 4
                nseg = (nblk * 128 + KSEG - 1) // KSEG
                bps = KSEG // 128
                items = []
                for h in range(8):
                    for sg in range(nseg):
                        nb = (min(nblk * 128, sg * KSEG + KSEG) - sg * KSEG) // 128
                        for j in range(nb):
                            items.append((h, sg, j, sg * bps + j))

                def loads(it):
                    h, sg, j, kb = it
                    sk = (h * nseg + sg) % 2
                    k0 = sg * KSEG
                    k1 = min(nblk * 128, k0 + KSEG)
                    nb = (k1 - k0) // 128
                    dma("sp", KN.t[:, sk, 0:k1 - k0], Kn_d[o, h, :, k0:k1], KN.u[sk], writes=[KN.u[sk]], dreads=[dKn[o]])
                    dma("sp", KP.t[0:64, sk, 0:k1 - k0], Kp_d[o, :, k0:k1], KP.u[sk], writes=[KP.u[sk]], dreads=[dKp[o]])
                    dma("sp", VV.t[:, sk, 0:nb * 128].rearrange("p (j v) -> p j v", v=128),
                        V_d[o, k0:k1, h * 128:(h + 1) * 128].rearrange("(j p) v -> p j v", p=128), VV.u[sk], writes=[VV.u[sk]], dreads=[dV[o]])

                def qk(it):
                    h, sg, j, kb = it
                    sk = (h * nseg + sg) % 2
                    d = kb - 4 * tile_idx
                    n0 = 128 * d if d > 0 else 0
                    pS = psum()
                    mm(pS, slice(0, 128), slice(n0, n), [(KN.t[:, sk, j * 128:(j + 1) * 128], A.t[:, h, n0:n]),
                                                       (KP.t[0:64, sk, j * 128:(j + 1) * 128], H.t[0:64, h, n0:n])],
                       [KN.u[sk], KP.u[sk], A.u[h], H.u[h]])
                    return pS

                def rest(it, pS):
                    h, sg, j, kb = it
                    sk = (h * nseg + sg) % 2
                    pO = PS[h % 2]
                    pL = PS[2 + h % 2]
                    d = kb - 4 * tile_idx
                    n0 = 128 * d if d > 0 else 0
                    pk = kb % 2
                    pt_ = PTb.t[:, pk, n0:n]
                    act(pt_, pS.t[:, n0:n], AF.Exp, pS.u, [PTb.u[pk]], scale=SCALE)
                    if d >= 0:
                        tt(PTb.t[:, pk, n0:n0 + 128], PTb.t[:, pk, n0:n0 + 128], TRIB.t[:, :], ALU.mult, [PTb.u[pk]] + TRIB.u, [PTb.u[pk]])
                    first = (kb == 0)
                    last = (kb == nblk - 1)
                    mm(pO, slice(0, 128), slice(n0, n), [(VV.t[:, sk, j * 128:(j + 1) * 128], pt_)], [VV.u[sk], PTb.u[pk]], start=first, stop=last)
                    mm(pL, slice(0, 128), slice(n0, n), [(ONESB.t[:, :], pt_)], [PTb.u[pk]] + ONESB.u, start=first, stop=last)
                    if last:
                        rl = STAT.t[:, 1 + h % 2, 0:n]
                        recip(rl, pL.t[:, 0:n], pL.u, [STAT.u[1 + h % 2]])
                        tt(OH(h), pO.t[:, 0:n], rl, ALU.mult, pO.u + [STAT.u[1 + h % 2]], [A.u[8 + h]])

                pst["n"] = 4
                loads(items[0])
                pS_next = qk(items[0])
                for i, it in enumerate(items):
                    pS_cur = pS_next
                    if i + 1 < len(items):
                        nx = items[i + 1]
                        if nx[2] == 0:
                            loads(nx)
                        pS_next = qk(nx)
                    rest(it, pS_cur)
                pst["n"] = 8
            else:
                sample_attn(o, n)
            for g4 in range(2):
                W, bw = ws.get(*wcols(w_out_c[o], g4 * 512, g4 * 512 + 512, 8))
                for m in range(4):
                    mc = g4 * 4 + m
                    ps = psum()
                    mm(ps, slice(0, 128), slice(0, n), [(W[:, k, m * 128:(m + 1) * 128], A.t[:, 8 + k, 0:n]) for k in range(8)], A.u[8:16] + [bw])
                    tt(X.t[:, mc, 0:n], X.t[:, mc, 0:n], ps.t[:, 0:n], ALU.add, [X.u[mc]] + ps.u, [X.u[mc]])

        def sample_attn(o, n):
            Wuk, buk = ws.get(w_uk[o].rearrange("(k p) c -> p k c", p=128), (128, 2, 1024))
            for cc in range(2):
                for h in range(8):
                    ps = psum()
                    mm(ps, slice(0, 128), slice(0, 128), [(Wuk[:, cc, h * 128:(h + 1) * 128], IDENTB.t[:, :])], [buk] + IDENTB.u)
                    copy(WUKT.t[:, h, cc * 128:(cc + 1) * 128], ps.t[:, 0:128], ps.u, WUKT.u)
            ql = QL.t[:, :].rearrange("p (c b h) -> p c b h", c=2, b=NS)
            for h in range(8):
                for cc in range(2):
                    ps = psum()
                    mm(ps, slice(0, 128), slice(0, n), [(WUKT.t[:, h, cc * 128:(cc + 1) * 128], A.t[:, h, 0:n])], WUKT.u + [A.u[h]])
                    copy(ql[:, cc, :, h], ps.t[:, 0:n], ps.u, QL.u)
            qps = QPS.t[:, :].rearrange("p (b h) -> p b h", h=8)
            for h in range(8):
                copy(qps[:, :, h], H.t[0:64, h, 0:n], [H.u[h]], QPS.u, eng="dve")
            olt = OLT.t[:, :].rearrange("p (c h b) -> p c h b", c=2, h=8)
            ngrp = NPG // GP
            items = [(b, pg) for b in range(NS) for pg in range(ngrp)]
            NI = len(items)

            def g_dma(s_):
                b, pg = items[s_]
                nk = b % 2
                gk = s_ % NCK
                if pg == 0:
                    ix = IDX.t[:, nk, :]
                    dma("sp", ix, ptab[0:1, b * NPG:(b + 1) * NPG].partition_broadcast(128), IDX.u[nk], writes=[IDX.u[nk]])
                    ts(ix, ix, 128.0, float(o * NPOOL * 128), ALU.mult, ALU.add, [IDX.u[nk]], [IDX.u[nk]])
                    ts(ix, ix, IOTAF.t[:, 0:1], None, ALU.add, None, [IDX.u[nk]] + IOTAF.u, [IDX.u[nk]])
                for i in range(GP):
                    col = pg * GP + i
                    P.dma("pool", lambda h_, gk=gk, i=i, col=col, nk=nk: h_.indirect_dma_start(
                        out=CK.t[:, gk, i * 324:i * 324 + 320], out_offset=None, in_=cpool[:, :],
                        in_offset=bass.IndirectOffsetOnAxis(ap=IDX.t[:, nk, col:col + 1], axis=0)),
                        CK.u[gk], reads=[IDX.u[nk]], writes=[CK.u[gk]])

            def g_T(s_):
                gk = s_ % NCK
                tk = s_ % 3
                for part in range(3):
                    ps = psum()
                    rows = 64 if part == 2 else 128
                    for i in range(GP):
                        src = CK.t[:, gk, i * 324 + part * 128:i * 324 + part * 128 + rows]
                        mm(ps, slice(0, rows), slice(i * 128, (i + 1) * 128), [(src, IDENTB.t[:, :])], [CK.u[gk]] + IDENTB.u)
                    copy(KT.t[0:rows, tk, part * GP * 128:(part + 1) * GP * 128], ps.t[0:rows, 0:GP * 128], ps.u, [KT.u[tk]],
                         eng=("act" if part != 1 else "dve"))

            def g_S(s_):
                b, pg = items[s_]
                tk = s_ % 3
                pk = s_ % 2
                pS = psum()
                for i in range(GP):
                    mm(pS, slice(0, 128), slice(i * 8, (i + 1) * 8),
                       [(KT.t[:, tk, i * 128:(i + 1) * 128], ql[:, 0, b, :]),
                        (KT.t[:, tk, GP * 128 + i * 128:GP * 128 + (i + 1) * 128], ql[:, 1, b, :]),
                        (KT.t[0:64, tk, 2 * GP * 128 + i * 128:2 * GP * 128 + (i + 1) * 128], qps[:, b, :])],
                       [KT.u[tk]] + QL.u + QPS.u)
                act(PSS.t[:, pk, 0:GP * 8], pS.t[:, 0:GP * 8], AF.Exp, pS.u, [PSS.u[pk]], scale=SCALE)

            def g_PV(s_):
                b, pg = items[s_]
                gk = s_ % NCK
                pk = s_ % 2
                nk = b % 2
                pO = PS[b % 2]
                pLs = PS[2 + b % 2]
                for i in range(GP):
                    st_ = (pg == 0 and i == 0)
                    mm(pO, slice(0, 8), slice(0, 256), [(PSS.t[:, pk, i * 8:(i + 1) * 8], CK.t[:, gk, i * 324:i * 324 + 256])],
                       [PSS.u[pk], CK.u[gk]], start=st_, stop=False)
                    mm(pLs, slice(0, 8), slice(0, 1), [(PSS.t[:, pk, i * 8:(i + 1) * 8], ONEC.t[:, 0:1])],
                       [PSS.u[pk]] + ONEC.u, start=st_, stop=False)
                if pg != ngrp - 1:
                    return
                dma("sp", NEWROW.t[0:1, nk, 0:256], TOK.t[b:b + 1, 0, 0:256], NEWROW.u[nk], reads=[TOK.u[0]], writes=[NEWROW.u[nk]])
                p1 = psum()
                mm(p1, slice(0, 1), slice(0, 8), [(A.t[:, 20, b:b + 1], ql[:, 0, b, :]), (A.t[:, 21, b:b + 1], ql[:, 1, b, :]),
                                                 (SQ.t[0:64, 0, b:b + 1], qps[:, b, :])], [A.u[20], A.u[21], SQ.u[0]] + QL.u + QPS.u)
                act(PN.t[0:1, nk, :], p1.t[0:1, 0:8], AF.Exp, p1.u, [PN.u[nk]], scale=SCALE)
                P.op("dve", lambda h_, nk=nk: h_.memset(NEWROW.t[0:1, nk, 256:257], 1.0), writes=[NEWROW.u[nk]])
                mm(pO, slice(0, 8), slice(0, 256), [(PN.t[0:1, nk, :], NEWROW.t[0:1, nk, 0:256])], [PN.u[nk], NEWROW.u[nk]], start=False, stop=True)
                mm(pLs, slice(0, 8), slice(0, 1), [(PN.t[0:1, nk, :], NEWROW.t[0:1, nk, 256:257])], [PN.u[nk], NEWROW.u[nk]], start=False, stop=True)
                recip(SM.t[0:8, nk, 0:1], pLs.t[0:8, 0:1], pLs.u, [SM.u[nk]])
                ts(OL.t[0:8, nk, :], pO.t[0:8, 0:256], SM.t[0:8, nk, 0:1], None, ALU.mult, None, pO.u + [SM.u[nk]], [OL.u[nk]])
                for cc in range(2):
                    ps = psum()
                    mm(ps, slice(0, 128), slice(0, 8), [(OL.t[0:8, nk, cc * 128:(cc + 1) * 128], IDENT.t[0:8, 0:8])], [OL.u[nk]] + IDENT.u)
                    copy(olt[:, cc, :, b], ps.t[:, 0:8], ps.u, OLT.u)

            pst["n"] = 4
            for s_ in range(min(2, NI)):
                g_dma(s_)
            for s_ in range(NI + 2):
                if s_ < NI:
                    g_T(s_)
                if 1 <= s_ <= NI:
                    g_S(s_ - 1)
                if s_ >= 2:
                    g_PV(s_ - 2)
                if s_ + 2 < NI:
                    g_dma(s_ + 2)
            pst["n"] = 8
            Wuv, buv = ws.get(w_uv[o].rearrange("(k p) c -> p k c", p=128), (128, 2, 1024))
            for h in range(8):
                ps = psum()
                mm(ps, slice(0, 128), slice(0, n), [(Wuv[:, cc, h * 128:(h + 1) * 128], olt[:, cc, h, :]) for cc in range(2)], OLT.u + [buv])
                copy(A.t[:, 8 + h, 0:n], ps.t[:, 0:n], ps.u, [A.u[8 + h]])

        def load_x(src, n):
            nt = max(1, n // 128)
            pt = min(n, 128)
            for q in range(nt):
                k = q % 2
                dma("sp", STG.t[0:pt, k, :], src[q * 128:q * 128 + pt, :], STG.u[k], writes=[STG.u[k]])
                for half in range(2):
                    ps = psum()
                    for i in range(4):
                        c = half * 4 + i
                        mm(ps, slice(0, 128), slice(i * 128, i * 128 + pt), [(STG.t[0:pt, k, c * 128:(c + 1) * 128], IDENT.t[0:pt, 0:pt])], [STG.u[k]] + IDENT.u)
                    for i in range(4):
                        c = half * 4 + i
                        copy(X.t[:, c, q * 128:q * 128 + pt], ps.t[:, i * 128:i * 128 + pt], ps.u, [X.u[c]], eng=("act" if half else "dve"))

        def final_norm(dst, n):
            nt = max(1, n // 128)
            pt = min(n, 128)
            ps = psum()
            for c in range(8):
                k = c % 2
                act(SQ.t[:, k, 0:n], X.t[:, c, 0:n], AF.Square, [X.u[c]], [SQ.u[k]])
                mm(ps, slice(0, 128), slice(0, n), [(ONESB.t[:, :], SQ.t[:, k, 0:n])], [SQ.u[k]] + ONESB.u, start=(c == 0), stop=(c == 7))
            rs = STAT.t[:, 0, 0:n]
            act(rs, ps.t[:, 0:n], AF.Sqrt, ps.u, [STAT.u[0]], scale=1.0 / D, bias=EPS)
            recip(rs, rs, [STAT.u[0]], [STAT.u[0]])
            for c in range(8):
                stt(ACC.t[:, c, 0:n], X.t[:, c, 0:n], VECT.t[:, c, 8:9], rs, ALU.mult, ALU.mult, [X.u[c], STAT.u[0]] + VECT.u, [ACC.u[c]])
            for q in range(nt):
                fm_to_tm(lambda c, q=q: ACC.t[:, c, q * 128:q * 128 + pt], lambda c: [ACC.u[c]], 8, pt, dst[q * 128:q * 128 + pt, :])

        import os
        KSET = os.environ.get("KSET", "abcdef")

        def setup():
            dma("sp", IDENT.t[:, :], c_ident, IDENT.u[0], writes=IDENT.u)
            if "a" in KSET:
                dma("sp", TRI.t[:, :], c_tri, TRI.u[0], writes=TRI.u)
                dma("sp", BSP.t[:, :], bsp, BSP.u[0], writes=BSP.u)
            if "b" in KSET:
                for o in range(2):
                    dma("sp", KVN.t[:, o, :], kvn[o:o + 1, :].partition_broadcast(128), KVN.u[0], writes=KVN.u)
                dma("sp", BC8.t[:, 0, :], w00.partition_broadcast(128), BC8.u[0], writes=BC8.u)
                dma("sp", BC8.t[:, 1, :], b00.partition_broadcast(128), BC8.u[0], writes=BC8.u)
            if "c" in KSET:
                copy(IDENTB.t[:, :], IDENT.t[:, :], IDENT.u, IDENTB.u, eng="dve")
                copy(TRIB.t[:, :], TRI.t[:, :], TRI.u, TRIB.u, eng="dve")
                P.op("dve", lambda h: h.memset(ONE1.t[:, :], 1.0), writes=ONE1.u)
                P.op("dve", lambda h: h.memset(CARRY.t[:, :, :], 0.0), writes=CARRY.u)
            P.op("dve", lambda h: h.memset(ONESB.t[:, :], 1.0), writes=ONESB.u)
            P.op("dve", lambda h: h.memset(ONEC.t[:, :], 1.0), writes=ONEC.u)
            if "d" in KSET:
                P.op("pool", lambda h: h.iota(IOTA.t[:, :], [[0, 1]], base=0, channel_multiplier=1), writes=IOTA.u)
                copy(IOTAF.t[:, :], IOTA.t[:, :], IOTA.u, IOTAF.u, eng="dve")
            dma("sp", STG.t[0:78, 0, :], vecs, STG.u[0], writes=[STG.u[0]])
            for half in range(2):
                ps = psum()
                for i in range(4):
                    c = half * 4 + i
                    mm(ps, slice(0, 128), slice(i * 128, i * 128 + 78), [(STG.t[0:78, 0, c * 128:(c + 1) * 128], IDENT.t[0:78, 0:78])], [STG.u[0]] + IDENT.u)
                for i in range(4):
                    c = half * 4 + i
                    copy(VECT.t[:, c, :], ps.t[:, i * 128:i * 128 + 78], ps.u, VECT.u)
            if "f" in KSET:
                for e in range(2):
                    for g in range(4):
                        k = (e * 4 + g) % 2
                        dma("sp", STG.t[:, k, 0:128], wsp[e, g], STG.u[k], writes=[STG.u[k]])
                        ps = psum()
                        mm(ps, slice(0, 128), slice(0, 128), [(STG.t[:, k, 0:128], IDENT.t[:, :])], [STG.u[k]] + IDENT.u)
                        tt(WST.t[:, e * 4 + g, :], ps.t[:, 0:128], TRI.t[:, :], ALU.mult, ps.u + TRI.u, WST.u)
                        ts(W00I.t[:, e * 4 + g, :], IDENT.t[0:16, 0:16], BC8.t[0:16, 0, e * 4 + g:e * 4 + g + 1], None, ALU.mult, None,
                           IDENT.u + BC8.u, W00I.u)

        DBG = os.environ.get("KDBG", "eofsxn")

        def program():
            setup()
            for t in range(NTILE + 1):
                sample = (t == NTILE)
                if sample and "s" not in DBG:
                    continue
                n = NS if sample else TT
                if "x" in DBG:
                    load_x(xs if sample else xp[t * TT:(t + 1) * TT, :], n)
                for L in range(4):
                    if L % 2 == 0:
                        if "e" in DBG:
                            even_layer(L // 2, L, n, sample, t)
                    else:
                        if "o" in DBG:
                            odd_layer(L // 2, L, n, sample, t)
                    if "f" in DBG:
                        ffn(L, n)
                if "n" in DBG:
                    final_norm(y_s if sample else y_p[t * TT:(t + 1) * TT, :], n)

        P.dry = True
        program()
        P.dry = False
        program()
        P.finish()
        with nc.Block() as block:
            P.emit(block)
    return nc


def _consts(SEQ, PAST):
    half = 32
    inv = (10000.0 ** (-np.arange(half, dtype=np.float32) / half)).astype(np.float32)
    pos = np.arange(SEQ, dtype=np.float32)
    ang = pos[:, None] * inv[None, :]
    cs_tm = np.concatenate([np.cos(ang), np.sin(ang)], axis=1).astype(np.float32)
    cs_fm = np.stack([np.concatenate([np.cos(ang).T, np.cos(ang).T], 0),
                      np.concatenate([np.sin(ang).T, np.sin(ang).T], 0)]).astype(np.float32)
    angs = (np.float32(PAST) * inv).astype(np.float32)
    cs_s = np.concatenate([np.cos(angs), np.sin(angs)])[None, :].astype(np.float32)
    cs_fm_s = np.stack([np.concatenate([np.cos(angs), np.cos(angs)]), np.concatenate([np.sin(angs), np.sin(angs)])]).astype(np.float32)
    ident = np.eye(128, dtype=np.float32)
    tri = np.triu(np.ones((128, 128), np.float32))
    return cs_tm, cs_fm, cs_s, cs_fm_s, ident, tri


_NC_CACHE = {}


def kernel(x_prompt, x_sample, cache_ckv, cache_kpe, page_table, state_conv,
           norm_mix, norm_ffn, norm_final,
           w_in_ab, gmlp_ln_g, gmlp_ln_b, w_spatial, b_spatial, conv_w, conv_b, conv_ln_g, conv_ln_b, w_out_ab,
           w_in_c, q_norm, kv_norm, w_uq, w_uk, w_uv, w_out_c,
           w_gate, w_up, w_down, _trace=False):
    f = lambda a: np.ascontiguousarray(np.asarray(a, dtype=np.float32))
    x_prompt, x_sample = f(x_prompt), f(x_sample)
    B, SEQ, _ = x_prompt.shape
    DB = x_sample.shape[0]
    NC = 8
    NS = DB // NC
    page_table = np.ascontiguousarray(np.asarray(page_table, dtype=np.int32))
    NPG = page_table.shape[1]
    NPOOL = cache_ckv.shape[1]
    PAST = NPG * cache_ckv.shape[2]
    key = (SEQ, NS, NPG, NPOOL)
    if key not in _NC_CACHE:
        _NC_CACHE[key] = build_nc(*key)
    nc = _NC_CACHE[key]
    cs_tm, cs_fm, cs_s, cs_fm_s, ident, tri = _consts(SEQ, PAST)
    conv_w, conv_b = f(conv_w), f(conv_b)
    q_norm = f(q_norm)
    vecs = np.concatenate([f(norm_mix), f(norm_ffn), f(norm_final)[None], conv_b, f(conv_ln_g), f(conv_ln_b),
                           conv_w.reshape(2 * 31, D), q_norm.reshape(1, D)], axis=0)
    assert vecs.shape == (78, D)
    w_spatial, b_spatial = f(w_spatial), f(b_spatial)
    shared = dict(
        cpool=np.concatenate([f(cache_ckv).reshape(2 * NPOOL * 128, 256), f(cache_kpe).reshape(2 * NPOOL * 128, 64)], axis=1),
        vecs=vecs, gmg=f(gmlp_ln_g), gmb=f(gmlp_ln_b), kvn=f(kv_norm), wsp=w_spatial,
        bsp=b_spatial.reshape(1, 2 * 512), w00=np.ascontiguousarray(w_spatial[:, :, 0, 0]).reshape(1, 8),
        b00=np.ascontiguousarray(b_spatial[:, :, 0]).reshape(1, 8),
        w_in_ab=f(w_in_ab), w_out_ab=f(w_out_ab), w_in_c=f(w_in_c), w_uq=f(w_uq),
        w_uk=f(w_uk).reshape(2, 256, 1024), w_uv=f(w_uv).reshape(2, 256, 1024), w_out_c=f(w_out_c),
        w_gate=f(w_gate), w_up=f(w_up), w_down=f(w_down),
        c_ident=ident, c_tri=tri, c_cs_tm=cs_tm, c_cs_s=cs_s, c_cs_fm=cs_fm,
        c_cs_fs=np.ascontiguousarray(np.repeat(cs_fm_s[:, :, None], NS, axis=2)),
    )
    state_conv = f(state_conv)
    zeros_x = np.zeros((SEQ, D), np.float32)
    in_maps = []
    for c in range(NC):
        m = dict(shared)
        m["xp"] = x_prompt[c] if c < B else zeros_x
        m["xs"] = x_sample[c * NS:(c + 1) * NS, 0, :]
        m["ptab"] = page_table[c * NS:(c + 1) * NS].reshape(1, NS * NPG)
        m["stc"] = np.ascontiguousarray(state_conv[:, c * NS:(c + 1) * NS]).reshape(2, NS * 30, D)
        in_maps.append(m)
    res = run_bass_kernel_spmd(nc, in_maps, core_ids=list(range(NC)), **({"trace": True} if _trace else {}))
    R = res.results
    y_prompt = np.stack([R[b]["y_p"] for b in range(B)])
    y_sample = np.concatenate([R[c]["y_s"] for c in range(NC)])[:, None, :]
    ckv_p = np.stack([R[b]["o_ckv_p"] for b in range(B)], axis=1)
    kpe_p = np.stack([R[b]["o_kpe_p"] for b in range(B)], axis=1)
    ckv_s = np.concatenate([R[c]["o_ckv_s"] for c in range(NC)], axis=1)[:, :, None, :]
    kpe_s = np.concatenate([R[c]["o_kpe_s"] for c in range(NC)], axis=1)[:, :, None, :]
    conv_p = np.stack([R[b]["o_conv_p"] for b in range(B)], axis=1)
    conv_s = np.concatenate([R[c]["o_conv_s"].reshape(2, NS, 30, D) for c in range(NC)], axis=1)
    v_s = np.concatenate([R[c]["o_v_s"] for c in range(NC)], axis=1)[:, :, None, :]
    if _trace:
        kernel.last_exec_ns = res.exec_time_ns
    return (y_prompt, y_sample, ckv_p, kpe_p, ckv_s, kpe_s, conv_p, conv_s, v_s)
```
